# Optimizing a Trainium2 kernel written in Bass

```python
import math
import jax, jax.numpy as jnp
from jax import lax
import numpy as np

D_MODEL = 2048
BATCH = 4
SEQ = 8192
DEPTH = 1

MEM_LEN = 256
D_MIX = D_MODEL
POOL_WIDTH = D_MIX // 4
POOL_WINDOWS = (2, 4, 8, 16)
POOL_GROUPS = len(POOL_WINDOWS)
POOL_GROUP_DIM = POOL_WIDTH // POOL_GROUPS
MLA_V_DIM = 128
MLA_HEADS = (D_MIX // 2) // MLA_V_DIM
MLA_NOPE_DIM = 128
MLA_ROPE_DIM = 64
MLA_QK_DIM = MLA_NOPE_DIM + MLA_ROPE_DIM
Q_LORA_RANK = 512
KV_LORA_RANK = 256
X_HEADS = 4
X_WIDTH = D_MIX // 4
X_HEAD_DIM = X_WIDTH // X_HEADS
D_FF = 5632
CONV_WIDTH = 3
ROPE_THETA = 10000.0
NORM_EPS = 1e-6
Q_BLOCK = 128

IN_COLS = POOL_WIDTH + Q_LORA_RANK + KV_LORA_RANK + MLA_ROPE_DIM + X_WIDTH
IN_SPLITS = (
    POOL_WIDTH,
    POOL_WIDTH + Q_LORA_RANK,
    POOL_WIDTH + Q_LORA_RANK + KV_LORA_RANK,
    POOL_WIDTH + Q_LORA_RANK + KV_LORA_RANK + MLA_ROPE_DIM,
)

kernel_name = "hybrid_pool_mla_memxattn_convglu"


def rms_norm(x, g):
    xf = x.astype(jnp.float32)
    y = xf * lax.rsqrt(jnp.mean(xf * xf, axis=-1, keepdims=True) + NORM_EPS)
    return (y * g.astype(jnp.float32)).astype(x.dtype)


def apply_rope(x, pos):
    half = x.shape[-1] // 2
    inv_freq = 1.0 / (ROPE_THETA ** (jnp.arange(half, dtype=jnp.float32) / half))
    ang = pos.astype(jnp.float32)[:, None] * inv_freq[None, :]
    cos = jnp.cos(ang)[None, :, None, :]
    sin = jnp.sin(ang)[None, :, None, :]
    xf = x.astype(jnp.float32)
    x1, x2 = xf[..., :half], xf[..., half:]
    return jnp.concatenate([x1 * cos - x2 * sin, x2 * cos + x1 * sin], axis=-1).astype(x.dtype)


def pool_mixer(p, w_pool, pool_scale):
    B, S, _ = p.shape
    pf = p.astype(jnp.float32).reshape(B, S, POOL_GROUPS, POOL_GROUP_DIM)
    csum = jnp.cumsum(pf, axis=1)
    t = jnp.arange(S)
    outs = []
    for gi, w in enumerate(POOL_WINDOWS):
        cg = csum[:, :, gi]
        lag = jnp.pad(cg, ((0, 0), (w, 0), (0, 0)))[:, :S]
        cnt = jnp.minimum(t + 1, w).astype(jnp.float32)[None, :, None]
        outs.append((cg - lag) / cnt - pf[:, :, gi])
    d = jnp.stack(outs, axis=2).astype(p.dtype)
    y = jnp.einsum('bsgc,gcd->bsgd', d, w_pool).reshape(B, S, POOL_WIDTH)
    return y * pool_scale


def causal_attention_blocks(q, k, v):
    B, S, H, D = q.shape
    Dv = v.shape[-1]
    nb = S // Q_BLOCK
    scale = 1.0 / math.sqrt(D)
    qb = q.reshape(B, nb, Q_BLOCK, H, D).transpose(1, 0, 2, 3, 4)
    kpos = jnp.arange(S)

    def one_block(args):
        qblk, i = args
        qpos = i * Q_BLOCK + jnp.arange(Q_BLOCK)
        s = jnp.einsum('bqhd,bkhd->bhqk', qblk, k).astype(jnp.float32) * scale
        mask = kpos[None, :] <= qpos[:, None]
        s = jnp.where(mask[None, None], s, -jnp.inf)
        pr = jax.nn.softmax(s, axis=-1).astype(v.dtype)
        return jnp.einsum('bhqk,bkhd->bqhd', pr, v)

    o = lax.map(one_block, (qb, jnp.arange(nb)))
    return o.transpose(1, 0, 2, 3, 4).reshape(B, S, H, Dv)


def mla_mixer(q_lat, kv_lat, k_rope, pos, g_q_lat, w_q_up, g_kv_lat, w_kv_up, g_q_mla, g_k_mla):
    B, S, _ = q_lat.shape
    q = (rms_norm(q_lat, g_q_lat) @ w_q_up).reshape(B, S, MLA_HEADS, MLA_QK_DIM)
    kv = (rms_norm(kv_lat, g_kv_lat) @ w_kv_up).reshape(B, S, MLA_HEADS, MLA_NOPE_DIM + MLA_V_DIM)
    k_nope, v = kv[..., :MLA_NOPE_DIM], kv[..., MLA_NOPE_DIM:]
    k_r = jnp.broadcast_to(k_rope[:, :, None, :], (B, S, MLA_HEADS, MLA_ROPE_DIM))
    k = jnp.concatenate([k_nope, k_r], axis=-1)
    q = rms_norm(q, g_q_mla)
    k = rms_norm(k, g_k_mla)
    q = jnp.concatenate([q[..., :MLA_NOPE_DIM], apply_rope(q[..., MLA_NOPE_DIM:], pos)], axis=-1)
    k = jnp.concatenate([k[..., :MLA_NOPE_DIM], apply_rope(k[..., MLA_NOPE_DIM:], pos)], axis=-1)
    o = causal_attention_blocks(q, k, v)
    return o.reshape(B, S, MLA_HEADS * MLA_V_DIM)


def memory_cross_attention(xq, mem, g_mem, w_mem_kv, g_q_x, g_k_x):
    B, S, _ = xq.shape
    M = mem.shape[1]
    q = rms_norm(xq.reshape(B, S, X_HEADS, X_HEAD_DIM), g_q_x)
    mkv = rms_norm(mem, g_mem) @ w_mem_kv
    k = rms_norm(mkv[..., :X_WIDTH].reshape(B, M, X_HEADS, X_HEAD_DIM), g_k_x)
    v = mkv[..., X_WIDTH:].reshape(B, M, X_HEADS, X_HEAD_DIM)
    s = jnp.einsum('bshd,bmhd->bhsm', q, k).astype(jnp.float32) * (1.0 / math.sqrt(X_HEAD_DIM))
    pr = jax.nn.softmax(s, axis=-1).astype(v.dtype)
    o = jnp.einsum('bhsm,bmhd->bshd', pr, v)
    return o.reshape(B, S, X_WIDTH)


def conv_glu_ffn(h, w_gate, w_up, conv_w, conv_b, w_down):
    S = h.shape[1]
    g = h @ w_gate
    gp = jnp.pad(g, ((0, 0), (CONV_WIDTH - 1, 0), (0, 0)))
    gc = conv_b
    for j in range(CONV_WIDTH):
        gc = gc + conv_w[j] * gp[:, j:j + S]
    return (jax.nn.silu(gc) * (h @ w_up)) @ w_down


def setup_inputs(seed: int = 0) -> dict:
    key = jax.random.key(seed)
    ks = jax.random.split(key, 24)
    f32 = jnp.float32
    L = DEPTH

    def nrm(k, shape, scale):
        return jax.random.normal(k, shape, f32) * scale

    def gain(k, n):
        return 1.0 + 0.02 * jax.random.normal(k, (L, n), f32)

    return {
        "x": nrm(ks[0], (BATCH, SEQ, D_MODEL), 1.0),
        "mem": nrm(ks[1], (BATCH, MEM_LEN, D_MODEL), 1.0),
        "g_mix": gain(ks[2], D_MODEL),
        "w_in": nrm(ks[3], (L, D_MODEL, IN_COLS), D_MODEL ** -0.5),
        "g_q_lat": gain(ks[4], Q_LORA_RANK),
        "w_q_up": nrm(ks[5], (L, Q_LORA_RANK, MLA_HEADS * MLA_QK_DIM), Q_LORA_RANK ** -0.5),
        "g_kv_lat": gain(ks[6], KV_LORA_RANK),
        "w_kv_up": nrm(ks[7], (L, KV_LORA_RANK, MLA_HEADS * (MLA_NOPE_DIM + MLA_V_DIM)), KV_LORA_RANK ** -0.5),
        "g_q_mla": gain(ks[8], MLA_QK_DIM),
        "g_k_mla": gain(ks[9], MLA_QK_DIM),
        "w_pool": nrm(ks[10], (L, POOL_GROUPS, POOL_GROUP_DIM, POOL_GROUP_DIM), POOL_GROUP_DIM ** -0.5),
        "pool_scale": 1.0 + 0.1 * jax.random.normal(ks[11], (L, POOL_WIDTH), f32),
        "g_mem": gain(ks[12], D_MODEL),
        "w_mem_kv": nrm(ks[13], (L, D_MODEL, 2 * X_WIDTH), D_MODEL ** -0.5),
        "g_q_x": gain(ks[14], X_HEAD_DIM),
        "g_k_x": gain(ks[15], X_HEAD_DIM),
        "w_o": nrm(ks[16], (L, D_MIX, D_MODEL), D_MIX ** -0.5),
        "g_ffn": gain(ks[17], D_MODEL),
        "w_gate": nrm(ks[18], (L, D_MODEL, D_FF), D_MODEL ** -0.5),
        "w_up": nrm(ks[19], (L, D_MODEL, D_FF), D_MODEL ** -0.5),
        "conv_w": nrm(ks[20], (L, CONV_WIDTH, D_FF), CONV_WIDTH ** -0.5),
        "conv_b": nrm(ks[21], (L, D_FF), 0.01),
        "w_down": nrm(ks[22], (L, D_FF, D_MODEL), D_FF ** -0.5),
    }


def reference(x, mem, g_mix, w_in, g_q_lat, w_q_up, g_kv_lat, w_kv_up, g_q_mla, g_k_mla,
              w_pool, pool_scale, g_mem, w_mem_kv, g_q_x, g_k_x, w_o, g_ffn,
              w_gate, w_up, conv_w, conv_b, w_down):
    S = x.shape[1]
    pos = jnp.arange(S)
    for l in range(DEPTH):
        h = rms_norm(x, g_mix[l])
        z = h @ w_in[l]
        z_pool, z_q, z_kv, z_kr, z_mq = jnp.split(z, IN_SPLITS, axis=-1)
        y_pool = pool_mixer(z_pool, w_pool[l], pool_scale[l])
        y_mla = mla_mixer(z_q, z_kv, z_kr, pos, g_q_lat[l], w_q_up[l], g_kv_lat[l],
                          w_kv_up[l], g_q_mla[l], g_k_mla[l])
        y_mem = memory_cross_attention(z_mq, mem, g_mem[l], w_mem_kv[l], g_q_x[l], g_k_x[l])
        x = x + jnp.concatenate([y_pool, y_mla, y_mem], axis=-1) @ w_o[l]
        x = x + conv_glu_ffn(rms_norm(x, g_ffn[l]), w_gate[l], w_up[l], conv_w[l], conv_b[l], w_down[l])
    return x
```

```python
import math
from contextlib import ExitStack

import numpy as np
import concourse.bass as bass
import concourse.mybir as mybir
from concourse.bass_utils import run_bass_kernel_spmd

F32 = mybir.dt.float32
BF16 = mybir.dt.bfloat16
ALU = mybir.AluOpType
AF = mybir.ActivationFunctionType

D = 2048
SEQ = 8192
NB = 4
MEM = 256
IN_COLS = 1856
DFF = 5632
NFT = DFF // 128
TK = 8192
OWN0 = 3968
NOWN = 4224
EPS = 1e-6
P1N = 256
C_POOL, C_Q, C_KV, C_KR, C_MQ = 0, 512, 1024, 1280, 1344


class Buf:
    __slots__ = ("name", "w", "r", "multi")

    def __init__(self, name, multi=False):
        self.name = name
        self.w = {}
        self.r = {}
        self.multi = multi


class Ev:
    __slots__ = ("key", "val", "clock")

    def __init__(self, key, val, clock):
        self.key = key
        self.val = val
        self.clock = clock


class V:
    __slots__ = ("ap", "bufs")

    def __init__(self, ap, bufs):
        self.ap = ap
        self.bufs = bufs


class Tile:
    def __init__(self, t, name, multi=False):
        self.t = t
        self.name = name
        self.multi = multi
        self.bufs = {}

    def b(self, k=None):
        if k not in self.bufs:
            self.bufs[k] = Buf(f"{self.name}.{k}", self.multi)
        return self.bufs[k]

    def v(self, idx=None, b=None):
        ap = self.t[:] if idx is None else self.t[idx]
        if b == "*":
            bufs = list(self.bufs.values())
        elif isinstance(b, (list, tuple)):
            bufs = [self.b(x) for x in b]
        else:
            bufs = [self.b(b)]
        return V(ap, bufs)


class Sched:
    EPOCH = 8000
    NDMA = 40

    def __init__(self, nc, stack):
        self.nc = nc
        self.stack = stack
        self.eng = {"pe": nc.tensor, "act": nc.scalar, "dve": nc.vector,
                    "pool": nc.gpsimd, "sp": nc.sync}
        self.count = {e: 0 for e in self.eng}
        self.clock = {e: {} for e in self.eng}
        self.sems = {e: [] for e in self.eng}
        self.dma_sems = [stack.enter_context(nc.semaphore(f"dq{i}")) for i in range(self.NDMA)]
        self.dma_val = [0] * self.NDMA
        self.dma_last = [None] * self.NDMA
        self.dma_i = 0
        self.nwaits = 0
        self.ninst = 0
        self.pending = {e: False for e in self.eng}

    def _sem(self, e, cnt):
        ep = (cnt - 1) // self.EPOCH
        while len(self.sems[e]) <= ep:
            self.sems[e].append(self.stack.enter_context(
                self.nc.semaphore(f"s_{e}_{len(self.sems[e])}")))
        return self.sems[e][ep], (cnt - 1) % self.EPOCH + 1

    def _wait_ev(self, e, ev):
        eng = self.eng[e]
        if isinstance(ev.key, tuple):
            eng.wait_ge(self.dma_sems[ev.key[1]], ev.val)
        else:
            sem, v = self._sem(ev.key, ev.val)
            eng.wait_ge(sem, v)
        self.nwaits += 1

    def _wait(self, e, deps):
        clk = self.clock[e]
        best = {}
        for ev in deps:
            if ev is None or clk.get(ev.key, 0) >= ev.val:
                continue
            o = best.get(ev.key)
            if o is None or o.val < ev.val:
                best[ev.key] = ev
        for ev in sorted(best.values(), key=lambda x: -len(x.clock)):
            if clk.get(ev.key, 0) >= ev.val:
                continue
            self._wait_ev(e, ev)
            for k, v in ev.clock.items():
                if clk.get(k, 0) < v:
                    clk[k] = v
            clk[ev.key] = ev.val

    def _deps(self, e, reads, writes):
        deps = []
        same_raw = e in ("act", "dve", "pool")
        for b in reads:
            for ev in b.w.values():
                if ev.key == e and not same_raw:
                    continue
                deps.append(ev)
        for b in writes:
            if b.multi:
                continue
            for ev in b.w.values():
                if ev.key != e:
                    deps.append(ev)
            for ev in b.r.values():
                if ev.key != e:
                    deps.append(ev)
        return deps

    def _record(self, ev, reads, writes):
        for b in reads:
            o = b.r.get(ev.key)
            if o is None or o.val < ev.val:
                b.r[ev.key] = ev
        for b in writes:
            if b.multi:
                o = b.w.get(ev.key)
                if o is None or o.val < ev.val:
                    b.w[ev.key] = ev
            else:
                b.w = {ev.key: ev}
                b.r = {}

    def op(self, e, fn, reads=(), writes=(), signal=True):
        rb = [b for v in reads for b in v.bufs]
        wb = [b for v in writes for b in v.bufs]
        self._wait(e, self._deps(e, rb, wb))
        inst = fn(self.eng[e])
        self.ninst += 1
        if signal:
            self.count[e] += 1
            sem, _ = self._sem(e, self.count[e])
            inst.then_inc(sem, 1)
            ev = Ev(e, self.count[e], dict(self.clock[e]))
            self.pending[e] = False
        else:
            ev = Ev(e, self.count[e] + 1, {})
            self.pending[e] = True
        self._record(ev, rb, wb)
        return ev

    def dma(self, q, out, in_, **kw):
        rb = list(in_.bufs)
        wb = list(out.bufs)
        deps = self._deps(q, rb, wb)
        i = self.dma_i
        self.dma_i = (i + 1) % self.NDMA
        deps.append(self.dma_last[i])
        self._wait(q, deps)
        self.dma_val[i] += 16
        self.eng[q].dma_start(out=out.ap, in_=in_.ap, **kw).then_inc(self.dma_sems[i], 16)
        self.ninst += 1
        ev = Ev(("d", i), self.dma_val[i], dict(self.clock[q]))
        self.dma_last[i] = ev
        self._record(ev, rb, wb)
        return ev

    def prewait(self, e, reads=(), writes=()):
        rb = [b for v in reads for b in v.bufs]
        wb = [b for v in writes for b in v.bufs]
        self._wait(e, self._deps(e, rb, wb))

    def barrier(self):
        assert not any(self.pending.values()), self.pending
        evs = [Ev(e, self.count[e], {}) for e in ("pe", "act", "dve", "pool") if self.count[e] > 0]
        evs += [x for x in self.dma_last if x is not None]
        for e in self.eng:
            self._wait(e, [x for x in evs if x.key != e])

    def finish(self):
        assert not any(self.pending.values()), self.pending
        self._wait("sp", [x for x in self.dma_last if x is not None])


class _Stop(Exception):
    pass


def build_nc(debug=None, stop=None):
    nc = bass.Bass("TRN2", target_bir_lowering=False)

    def ck(name):
        if stop == name:
            raise _Stop()

    def din(name, shape, dt=F32):
        return Tile(nc.dram_tensor(name, shape, dt, kind="ExternalInput").ap(), name, multi=True)

    def dscr(name, shape, dt):
        return Tile(nc.dram_tensor(name, shape, dt).ap(), name, multi=True)

    xk = din("xk", [TK, D])
    mem = din("mem", [MEM, D])
    g_mix = din("g_mix", [D]); g_ffn = din("g_ffn", [D]); g_mem = din("g_mem", [D])
    g_q_lat = din("g_q_lat", [512]); g_kv_lat = din("g_kv_lat", [256])
    g_q_mla = din("g_q_mla", [192]); g_k_mla = din("g_k_mla", [192])
    pool_scale = din("pool_scale", [512]); g_q_x = din("g_q_x", [128]); g_k_x = din("g_k_x", [128])
    conv_w = din("conv_w", [3, DFF]); conv_b = din("conv_b", [DFF])
    w_in = din("w_in", [D, IN_COLS]); w_q_up = din("w_q_up", [512, 1536]); w_kv_up = din("w_kv_up", [256, 2048])
    w_pool = din("w_pool", [4, 128, 128]); w_mem_kv = din("w_mem_kv", [D, 1024]); w_o = din("w_o", [D, D])
    w_gate = din("w_gate", [D, DFF]); w_up = din("w_up", [D, DFF]); w_down = din("w_down", [DFF, D])
    cosT = din("cosT", [64, TK]); sinT = din("sinT", [64, TK])
    kbias_d = din("kbias", [128, 64]); halo_d = din("halo_flag", [128, 1]); pinv_d = din("pool_inv", [128, 4, P1N])
    ident_d = din("ident_f", [128, 128]); rot_d = din("rotT", [64, 64]); tri_d = din("tri", [128, 128])
    out_t = Tile(nc.dram_tensor("out", [4096, D], F32, kind="ExternalOutput").ap(), "out", multi=True)

    qn_d = dscr("qn_d", [128, 4, NOWN], BF16)
    cat_d = dscr("cat_d", [128, 16, NOWN], BF16)
    x1_d = dscr("x1_d", [NOWN, D], F32)
    wg_b = dscr("wg_b", [NFT, 128, 16, 128], BF16)
    wu_b = dscr("wu_b", [NFT, 128, 16, 128], BF16)
    wd_b = dscr("wd_b", [NFT, 128, D], BF16)
    dbg = None
    if debug:
        dbg = Tile(nc.dram_tensor("dbg", list(debug), F32, kind="ExternalOutput").ap(), "dbg", multi=True)

    with ExitStack() as st:
        S = Sched(nc, st)
        try:
            _program(nc, st, S, ck, locals())
        except _Stop:
            S.finish()
            st.pop_all()
        print("instructions", S.ninst, "waits", S.nwaits, {e: S.count[e] for e in S.count})
    return nc


def _program(nc, st, S, ck, G):
    globals_ = G
    xk = G["xk"]; mem = G["mem"]; g_mix = G["g_mix"]; g_ffn = G["g_ffn"]; g_mem = G["g_mem"]
    g_q_lat = G["g_q_lat"]; g_kv_lat = G["g_kv_lat"]; g_q_mla = G["g_q_mla"]; g_k_mla = G["g_k_mla"]
    pool_scale = G["pool_scale"]; g_q_x = G["g_q_x"]; g_k_x = G["g_k_x"]; conv_w = G["conv_w"]; conv_b = G["conv_b"]
    w_in = G["w_in"]; w_q_up = G["w_q_up"]; w_kv_up = G["w_kv_up"]; w_pool = G["w_pool"]; w_mem_kv = G["w_mem_kv"]
    w_o = G["w_o"]; w_gate = G["w_gate"]; w_up = G["w_up"]; w_down = G["w_down"]; cosT = G["cosT"]; sinT = G["sinT"]
    kbias_d = G["kbias_d"]; halo_d = G["halo_d"]; pinv_d = G["pinv_d"]; ident_d = G["ident_d"]; rot_d = G["rot_d"]
    tri_d = G["tri_d"]; out_t = G["out_t"]; qn_d = G["qn_d"]; cat_d = G["cat_d"]; x1_d = G["x1_d"]
    wg_b = G["wg_b"]; wu_b = G["wu_b"]; wd_b = G["wd_b"]; dbg = G["dbg"]; debug = G["debug"]
    if True:

        def sb(stack, shape, dt, name, multi=False):
            return Tile(stack.enter_context(nc.sbuf_tensor("sb_" + name, shape, dt)), name, multi)

        def mm(o, l, r, start=True, stop=True, signal=True):
            return S.op("pe", lambda e: e.matmul(o.ap, lhsT=l.ap, rhs=r.ap, start=start, stop=stop),
                        reads=[l, r], writes=[o], signal=signal)

        def tr(o, i, ident, signal=True):
            return S.op("pe", lambda e: e.transpose(out=o.ap, in_=i.ap, identity=ident.ap),
                        reads=[i, ident], writes=[o], signal=signal)

        def act(o, i, func, scale=1.0, bias=None, accum=None):
            rd = [i]
            kw = {}
            if isinstance(scale, V):
                rd.append(scale); kw["scale"] = scale.ap
            else:
                kw["scale"] = scale
            if bias is not None:
                if isinstance(bias, V):
                    rd.append(bias); kw["bias"] = bias.ap
                else:
                    kw["bias"] = bias
            wr = [o]
            if accum is not None:
                wr.append(accum); kw["accum_out"] = accum.ap
            return S.op("act", lambda e: e.activation(out=o.ap, in_=i.ap, func=func, **kw), reads=rd, writes=wr)

        def tt(eng, o, a, b_, op):
            return S.op(eng, lambda e: e.tensor_tensor(out=o.ap, in0=a.ap, in1=b_.ap, op=op), reads=[a, b_], writes=[o])

        def ts(eng, o, a, s1, op0, s2=None, op1=None):
            rd = [a]
            if isinstance(s1, V):
                rd.append(s1); s1a = s1.ap
            else:
                s1a = s1
            if isinstance(s2, V):
                rd.append(s2); s2a = s2.ap
            else:
                s2a = s2
            kw = {} if op1 is None else {"op1": op1}
            return S.op(eng, lambda e: e.tensor_scalar(out=o.ap, in0=a.ap, scalar1=s1a, scalar2=s2a, op0=op0, **kw),
                        reads=rd, writes=[o])

        def stt(eng, o, a, s, b_, op0, op1):
            rd = [a, b_]
            if isinstance(s, V):
                rd.append(s); sa = s.ap
            else:
                sa = s
            return S.op(eng, lambda e: e.scalar_tensor_tensor(out=o.ap, in0=a.ap, scalar=sa, in1=b_.ap, op0=op0, op1=op1),
                        reads=rd, writes=[o])

        def cp(eng, o, i):
            if eng == "act":
                return S.op("act", lambda e: e.copy(out=o.ap, in_=i.ap), reads=[i], writes=[o])
            return S.op(eng, lambda e: e.tensor_copy(out=o.ap, in_=i.ap), reads=[i], writes=[o])

        def memset(eng, o, val):
            return S.op(eng, lambda e: e.memset(o.ap, val), writes=[o])

        def recip(o, i):
            return S.op("dve", lambda e: e.reciprocal(out=o.ap, in_=i.ap), reads=[i], writes=[o])

        def rsqrt_act(o, i, inv_n, eps_v):
            act(o, i, AF.Ln, scale=inv_n, bias=eps_v)
            act(o, o, AF.Exp, scale=-0.5)

        pb = [Tile(st.enter_context(nc.psum_tensor(f"pb{i}", [128, 512], F32)), f"pb{i}") for i in range(8)]

        def pbf(i):
            return pb[i].t[:].bitcast(BF16).rearrange("p (k c) -> p k c", c=128)

        idf = sb(st, [128, 128], F32, "idf")
        idb = sb(st, [128, 128], BF16, "idb")
        ones = sb(st, [128, 128], BF16, "ones")
        rotT = sb(st, [64, 64], F32, "rotT")
        tri = sb(st, [128, 128], BF16, "tri")
        epsT = sb(st, [128, 1], F32, "epsT")
        kbias = sb(st, [128, 64], F32, "kbias")
        halo = sb(st, [128, 1], F32, "halo")
        NCOL = 16 * 3 + 4 + 2 + 4 + 4 + 4 + 3 * NFT + NFT
        cols = sb(st, [128, NCOL], F32, "cols")
        O_GMIX, O_GFFN, O_GMEM = 0, 16, 32
        O_GQL, O_GKVL = 48, 52
        O_GQN, O_GQR, O_GKN, O_GKR = 54, 55, 56, 57
        O_PS = 58
        O_GQX, O_GKX = 62, 63
        O_CW = 66
        O_CB = 66 + 3 * NFT

        def col(o, n=1, p=128):
            return cols.v((slice(0, p), slice(o, o + n)))

        S.dma("sp", idf.v(), ident_d.v())
        cp("dve", idb.v(), idf.v())
        memset("dve", ones.v(), 1.0)
        memset("dve", epsT.v(), EPS)
        S.dma("sp", rotT.v(), rot_d.v())
        S.dma("sp", kbias.v(), kbias_d.v())
        S.dma("sp", halo.v(), halo_d.v())

        with ExitStack() as s0:
            stg = sb(s0, [128, 128], F32, "stg")
            S.dma("sp", stg.v(), tri_d.v())
            cp("dve", tri.v(), stg.v())

            def vec_cols(vec_ap, r, w, off):
                S.dma("sp", stg.v((slice(0, r), slice(0, w))), V(vec_ap, []))
                tr(V(pb[0].t[0:w, 0:r], [pb[0].b()]), stg.v((slice(0, r), slice(0, w))), idf.v((slice(0, r), slice(0, r))))
                cp("dve", cols.v((slice(0, w), slice(off, off + r))), V(pb[0].t[0:w, 0:r], [pb[0].b()]))

            def r128(t):
                return t.t.rearrange("(r p) -> r p", p=128)

            vec_cols(r128(g_mix), 16, 128, O_GMIX)
            vec_cols(r128(g_ffn), 16, 128, O_GFFN)
            vec_cols(r128(g_mem), 16, 128, O_GMEM)
            vec_cols(r128(g_q_lat), 4, 128, O_GQL)
            vec_cols(r128(g_kv_lat), 2, 128, O_GKVL)
            vec_cols(g_q_mla.t[0:128].rearrange("(r p) -> r p", p=128), 1, 128, O_GQN)
            vec_cols(g_q_mla.t[128:192].rearrange("(r p) -> r p", p=64), 1, 64, O_GQR)
            vec_cols(g_k_mla.t[0:128].rearrange("(r p) -> r p", p=128), 1, 128, O_GKN)
            vec_cols(g_k_mla.t[128:192].rearrange("(r p) -> r p", p=64), 1, 64, O_GKR)
            vec_cols(r128(pool_scale), 4, 128, O_PS)
            vec_cols(r128(g_q_x), 1, 128, O_GQX)
            vec_cols(r128(g_k_x), 1, 128, O_GKX)
            for j in range(3):
                vec_cols(conv_w.t[j].rearrange("(r p) -> r p", p=128), NFT, 128, O_CW + j * NFT)
            vec_cols(r128(conv_b), NFT, 128, O_CB)
            S.barrier()
        ck("p0")

        p12 = st.enter_context(ExitStack())
        nT = sb(p12, [128, 2, TK], BF16, "nT", multi=True)
        krT = sb(p12, [128, TK], BF16, "krT", multi=True)
        S.op("pool", lambda e: e.memset(krT.t[64:128, :], 0.0), writes=[V(None, [krT.b()])])
        sskr = sb(p12, [128, 64], F32, "sskr", multi=True)

        cast_jobs = []
        for wsrc, wdst in ((w_gate, wg_b), (w_up, wu_b)):
            for k in range(16):
                for q in range(4):
                    cast_jobs.append(("gu", wsrc, wdst, k, q))
        for j in range(NFT):
            for hh in range(2):
                cast_jobs.append(("d", w_down, wd_b, j, hh))
        cast_pos = [0]

        with ExitStack() as s1:
            winb = sb(s1, [128, 16, IN_COLS], BF16, "winb")
            xts = [sb(s1, [128, D], F32, f"xt{i}") for i in range(2)]
            xb = sb(s1, [128, D], BF16, "xb")
            ssx = sb(s1, [128, 1], F32, "ssx")
            rsx = sb(s1, [128, 1], F32, "rsx")
            hT = sb(s1, [128, 16, P1N], BF16, "hT")
            zts = [sb(s1, [128, P1N], F32, f"zt{i}") for i in range(4)]
            sqs = [sb(s1, [128, P1N], BF16, f"sq{i}") for i in range(2)]
            rss = [sb(s1, [128, P1N], F32, f"rs{i}") for i in range(2)]
            pbufs = [sb(s1, [128, 16 + P1N], F32, f"pbuf{i}") for i in range(4)]
            phist = [sb(s1, [128, 16], F32, f"phist{i}") for i in range(4)]
            ptmp = [sb(s1, [128, 16 + P1N], F32, f"ptmp{i}") for i in range(2)]
            dbf = [sb(s1, [128, P1N], BF16, f"dbf{i}") for i in range(2)]
            ycat = [sb(s1, [128, P1N], BF16, f"ycat{i}") for i in range(2)]
            ycx = [sb(s1, [128, P1N], BF16, f"ycx{i}") for i in range(2)]
            qn = sb(s1, [128, 4, P1N], BF16, "qn")
            KxT = sb(s1, [128, 4, MEM], BF16, "KxT")
            Vx = sb(s1, [128, 2, 512], BF16, "Vx")
            qx = sb(s1, [128, P1N], BF16, "qx")
            pexp = [sb(s1, [128, P1N], BF16, f"pexp{i}") for i in range(2)]
            rl1 = sb(s1, [128, P1N], F32, "rl1")
            kg = sb(s1, [64, P1N], F32, "kg")
            cs_t = sb(s1, [64, P1N], F32, "cs_t")
            sn_t = sb(s1, [64, P1N], F32, "sn_t")
            t1 = sb(s1, [64, P1N], F32, "t1")
            t2 = sb(s1, [64, P1N], F32, "t2")
            pinv = sb(s1, [128, 4, P1N], F32, "pinv")
            wpool = sb(s1, [128, 4, 128], BF16, "wpool")
            S.dma("sp", pinv.v(), pinv_d.v())
            for g in range(4):
                memset("pool", phist[g].v(), 0.0)

            zi = [0]

            def next_zt():
                zi[0] += 1
                return zts[zi[0] % 4]

            pz_i = [0]

            def next_pz():
                pz_i[0] += 1
                return pb[2 + pz_i[0] % 4]

            xbs = [xb]
            hTs = [hT]
            curb = {"hT": hT, "xb": xb}

            def norm_pre(src_tile, row0, xt, xb):
                S.dma("sp", xt.v(), V(src_tile.t[row0:row0 + 128, :], []))
                act(xb.v(), xt.v(), AF.Square, accum=ssx.v())
                rsqrt_act(rsx.v(), ssx.v(), 1.0 / D, epsT.v())
                ts("dve", xb.v(), xt.v(), rsx.v(), ALU.mult)

            def norm_post(xb):
                for k in range(16):
                    bank = pb[k // 8]
                    tr(V(pbf(k // 8)[:, k % 8, :], [bank.b()]), xb.v((slice(None), slice(k * 128, (k + 1) * 128))),
                       idb.v(), signal=(k % 8 == 7))

            def norm_block(src_tile, row0, xt, gain_none=True):
                norm_pre(src_tile, row0, xt, curb["xb"])
                norm_post(curb["xb"])

            def evac_hT(dst, c0):
                cp("dve", V(dst.t[:, 0:8, c0:c0 + 128], [dst.b(("a", c0))]), V(pbf(0), [pb[0].b()]))
                cp("act", V(dst.t[:, 8:16, c0:c0 + 128], [dst.b(("b", c0))]), V(pbf(1), [pb[1].b()]))

            def hT_rhs(k, n):
                hT = curb["hT"]
                return V(hT.t[:, k, 0:n], [hT.b((("a" if k < 8 else "b"), c0)) for c0 in range(0, n, 128)])

            def proj(c0, wd, n, evac_eng):
                pz = next_pz()
                for k in range(16):
                    mm(V(pz.t[0:wd, 0:n], [pz.b()]), V(winb.t[:, k, c0:c0 + wd], [winb.b(k)]), hT_rhs(k, n),
                       start=(k == 0), stop=(k == 15), signal=(k == 15))
                z = next_zt()
                cp(evac_eng, z.v((slice(0, wd), slice(0, n))), V(pz.t[0:wd, 0:n], [pz.b()]))
                return z

            def stage_weight(s_, dst_fn, src_rows, ncols, k_tiles, gain_off, tag):
                stgs = [sb(s_, [128, ncols], F32, f"wst_{tag}{i}") for i in range(2)]
                for k in range(k_tiles):
                    stg_ = stgs[k % 2]
                    S.dma("sp", stg_.v(), V(src_rows[k], []))
                    if gain_off is None:
                        cp("dve", dst_fn(k), stg_.v())
                    else:
                        ts("dve", dst_fn(k), stg_.v(), col(gain_off + k), ALU.mult)

            with ExitStack() as s0:
                stage_weight(s0, lambda k: V(winb.t[:, k, 0:1024], [winb.b(k)]),
                             w_mem_kv.t.rearrange("(k p) c -> k p c", p=128), 1024, 16, O_GMEM, "m")
                stgp = sb(s0, [128, 4, 128], F32, "wst_p")
                S.dma("sp", stgp.v(), V(w_pool.t.rearrange("g c d -> c g d"), []))
                cp("dve", wpool.v(), stgp.v())
                for mb in range(2):
                    norm_block(mem, mb * 128, xts[mb])
                    evac_hT(hT, mb * 128)
                for h in range(4):
                    z = proj(h * 128, 128, MEM, "act")
                    sq = sqs[h % 2]
                    act(sq.v(), z.v(), AF.Square)
                    mm(V(pb[6].t[:, 0:MEM], [pb[6].b()]), ones.v(), sq.v())
                    rs = rss[h % 2]
                    rsqrt_act(rs.v(), V(pb[6].t[:, 0:MEM], [pb[6].b()]), 1.0 / 128, epsT.v())
                    stt("dve", V(KxT.t[:, h, :], [KxT.b(h)]), z.v(), col(O_GKX), rs.v(), ALU.mult, ALU.mult)
                for mb in range(2):
                    pz = next_pz()
                    for k in range(16):
                        mm(V(pz.t[:, 0:512], [pz.b()]),
                           V(hT.t[:, k, mb * 128:(mb + 1) * 128], [hT.b((("a" if k < 8 else "b"), mb * 128))]),
                           V(winb.t[:, k, 512:1024], [winb.b(k)]), start=(k == 0), stop=(k == 15), signal=(k == 15))
                    cp("act", V(Vx.t[:, mb, :], [Vx.b(mb)]), V(pz.t[:, 0:512], [pz.b()]))
                stage_weight(s0, lambda k: V(winb.t[:, k, :], [winb.b(k)]),
                             w_in.t.rearrange("(k p) c -> k p c", p=128), IN_COLS, 16, O_GMIX, "i")
                S.barrier()
            ck("mem")
            hTs.append(sb(s1, [128, 16, P1N], BF16, "hT2"))
            xbs.append(sb(s1, [128, D], BF16, "xb2"))

            chunks = [(0, 128)] + [(128 + P1N * i, P1N) for i in range(15)] + [(OWN0, 128)] + \
                     [(OWN0 + 128 + P1N * i, P1N) for i in range(16)]
            assert chunks[15][0] + chunks[15][1] == OWN0 and chunks[-1][0] + chunks[-1][1] == TK
            blk_i = 0
            for ci, (s, n) in enumerate(chunks):
                own = s >= OWN0
                os_ = s - OWN0
                first_real = (s == OWN0 + 128)
                nb_ = n // 128
                curb["hT"] = hTs[ci % 2]
                if ci == 0:
                    for b_ in range(nb_):
                        norm_pre(xk, s + b_ * 128, xts[b_], xbs[b_])
                    for b_ in range(nb_):
                        norm_post(xbs[b_])
                        evac_hT(hTs[0], b_ * 128)
                if ci + 1 < len(chunks):
                    s_n, n_n = chunks[ci + 1]
                    for b_ in range(n_n // 128):
                        norm_pre(xk, s_n + b_ * 128, xts[b_], xbs[b_])
                S.dma("sp", cs_t.v((slice(None), slice(0, n))), V(cosT.t[:, s:s + n], []))
                S.dma("sp", sn_t.v((slice(None), slice(0, n))), V(sinT.t[:, s:s + n], []))
                sl = (slice(None), slice(0, n))
                zkv = [proj(C_KV + t * 128, 128, n, "act" if t == 0 else "dve") for t in range(2)]
                for t in range(2):
                    act(sqs[t].v(sl), zkv[t].v(sl), AF.Square)
                    mm(V(pb[6].t[:, 0:n], [pb[6].b()]), ones.v(), sqs[t].v(sl), start=(t == 0), stop=(t == 1),
                       signal=(t == 1))
                rsqrt_act(rss[0].v(sl), V(pb[6].t[:, 0:n], [pb[6].b()]), 1.0 / 256, epsT.v())
                for t in range(2):
                    stt("dve", V(nT.t[:, t, s:s + n], [nT.b()]), zkv[t].v(sl), col(O_GKVL + t), rss[0].v(sl),
                        ALU.mult, ALU.mult)
                zkr = proj(C_KR, 64, n, "act")
                sl64 = (slice(0, 64), slice(0, n))
                act(sqs[0].v(sl64), zkr.v(sl64), AF.Square)
                for b_ in range(nb_):
                    mm(V(pb[7].t[:, b_:b_ + 1], [pb[7].b()]), sqs[0].v((slice(0, 64), slice(b_ * 128, (b_ + 1) * 128))),
                       ones.v((slice(0, 64), slice(0, 1))), signal=(b_ == nb_ - 1))
                cp("dve", V(sskr.t[:, s // 128:s // 128 + nb_], [sskr.b()]), V(pb[7].t[:, 0:nb_], [pb[7].b()]))
                ts("dve", kg.v(sl64), zkr.v(sl64), col(O_GKR, 1, 64), ALU.mult)
                mm(V(pb[7].t[0:64, 0:n], [pb[7].b()]), rotT.v(), kg.v(sl64))
                tt("dve", t1.v(sl64), kg.v(sl64), cs_t.v(sl64), ALU.mult)
                tt("dve", t2.v(sl64), V(pb[7].t[0:64, 0:n], [pb[7].b()]), sn_t.v(sl64), ALU.mult)
                tt("dve", V(krT.t[0:64, s:s + n], [krT.b()]), t1.v(sl64), t2.v(sl64), ALU.add)
                if own:
                    def pool_gen(g):
                        w = 2 << g
                        pz = next_pz()
                        for k in range(16):
                            mm(V(pz.t[:, 0:n], [pz.b()]), V(winb.t[:, k, C_POOL + g * 128:C_POOL + (g + 1) * 128], [winb.b(k)]),
                               hT_rhs(k, n), start=(k == 0), stop=(k == 15), signal=(k == 15))
                        yield
                        pbuf = pbufs[g]
                        L = 16 + n
                        cp("pool", pbuf.v((slice(None), slice(0, 16))), phist[g].v())
                        cp("act", pbuf.v((slice(None), slice(16, L))), V(pz.t[:, 0:n], [pz.b()]))
                        cp("pool", phist[g].v(), pbuf.v((slice(None), slice(n, L))))
                        yield
                        cur = pbuf
                        sh = 1
                        lo = 0
                        for lvl in range(g + 1):
                            lo2 = lo + sh
                            dst = ptmp[lvl % 2]
                            tt("pool", dst.v((slice(None), slice(lo2, L))), cur.v((slice(None), slice(lo2, L))),
                               cur.v((slice(None), slice(lo, L - sh))), ALU.add)
                            cur = dst
                            lo = lo2
                            sh *= 2
                            yield
                        d_ = dbf[g % 2]
                        if first_real:
                            tt("dve", cur.v((slice(None), slice(16, L))), cur.v((slice(None), slice(16, L))),
                               V(pinv.t[:, g, 0:n], [pinv.b()]), ALU.mult)
                            tt("dve", d_.v(sl), cur.v((slice(None), slice(16, L))), pbuf.v((slice(None), slice(16, L))),
                               ALU.subtract)
                        else:
                            stt("dve", d_.v(sl), cur.v((slice(None), slice(16, L))), 1.0 / w,
                                pbuf.v((slice(None), slice(16, L))), ALU.mult, ALU.subtract)
                        yield
                        pz2 = next_pz()
                        mm(V(pz2.t[:, 0:n], [pz2.b()]), V(wpool.t[:, g, :], [wpool.b()]), d_.v(sl))
                        yield
                        yc = ycat[g % 2]
                        act(yc.v(sl), V(pz2.t[:, 0:n], [pz2.b()]), AF.Copy, scale=col(O_PS + g))
                        S.dma("sp", V(cat_d.t[:, g, os_:os_ + n], [cat_d.b()]), yc.v(sl))

                    ck(f"p1_c{ci}_pool")
                    zq = []
                    for t in range(4):
                        z = proj(C_Q + t * 128, 128, n, "act" if t % 2 == 0 else "dve")
                        zq.append(z)
                        act(sqs[t % 2].v(sl), z.v(sl), AF.Square)
                        mm(V(pb[6].t[:, 0:n], [pb[6].b()]), ones.v(), sqs[t % 2].v(sl), start=(t == 0), stop=(t == 3),
                           signal=(t == 3))
                    rsqrt_act(rss[1].v(sl), V(pb[6].t[:, 0:n], [pb[6].b()]), 1.0 / 512, epsT.v())
                    for t in range(4):
                        stt("dve", V(qn.t[:, t, 0:n], [qn.b()]), zq[t].v(sl), col(O_GQL + t), rss[1].v(sl),
                            ALU.mult, ALU.mult)
                    S.dma("sp", V(qn_d.t[:, :, os_:os_ + n], [qn_d.b()]), V(qn.t[:, :, 0:n], [qn.b()]))
                    ck(f"p1_c{ci}_q")
                    def xattn_gen(h):
                        z = proj(C_MQ + h * 128, 128, n, "act")
                        yield
                        act(sqs[h % 2].v(sl), z.v(sl), AF.Square)
                        mm(V(pb[6].t[:, 0:n], [pb[6].b()]), ones.v(), sqs[h % 2].v(sl))
                        yield
                        rs = rss[h % 2]
                        rsqrt_act(rs.v(sl), V(pb[6].t[:, 0:n], [pb[6].b()]), 1.0 / 128, epsT.v())
                        yield
                        stt("dve", qx.v(sl), z.v(sl), col(O_GQX), rs.v(sl), ALU.mult, ALU.mult)
                        yield
                        for mb in range(2):
                            pz = next_pz()
                            mm(V(pz.t[:, 0:n], [pz.b()]), V(KxT.t[:, h, mb * 128:(mb + 1) * 128], [KxT.b(h)]), qx.v(sl))
                            act(pexp[mb].v(sl), V(pz.t[:, 0:n], [pz.b()]), AF.Exp, scale=1.0 / math.sqrt(128.0))
                        yield
                        po_ = next_pz()
                        for mb in range(2):
                            mm(V(po_.t[:, 0:n], [po_.b()]), V(Vx.t[:, mb, h * 128:(h + 1) * 128], [Vx.b(mb)]), pexp[mb].v(sl),
                               start=(mb == 0), stop=(mb == 1), signal=(mb == 1))
                        for mb in range(2):
                            mm(V(pb[6].t[:, 0:n], [pb[6].b()]), ones.v(), pexp[mb].v(sl),
                               start=(mb == 0), stop=(mb == 1), signal=(mb == 1))
                        yield
                        recip(rl1.v(sl), V(pb[6].t[:, 0:n], [pb[6].b()]))
                        yield
                        yc = ycx[h % 2]
                        tt("dve", yc.v(sl), V(po_.t[:, 0:n], [po_.b()]), rl1.v(sl), ALU.mult)
                        S.dma("sp", V(cat_d.t[:, 12 + h, os_:os_ + n], [cat_d.b()]), yc.v(sl))

                    for g in range(4):
                        gens = [pool_gen(g), xattn_gen(g)]
                        while gens:
                            for gg in list(gens):
                                try:
                                    next(gg)
                                except StopIteration:
                                    gens.remove(gg)
                if ci + 1 < len(chunks):
                    s_n, n_n = chunks[ci + 1]
                    for b_ in range(n_n // 128):
                        norm_post(xbs[b_])
                        evac_hT(hTs[(ci + 1) % 2], b_ * 128)
                ck(f"p1_c{ci}")
            ck("p1")
            if debug == "p1":
                pass
            S.barrier()

        SC = 1.0 / math.sqrt(192.0)
        with ExitStack() as s2:
            wq_b = sb(s2, [128, 4, 1536], BF16, "wq_b")
            wkv_b = sb(s2, [128, 2, 2048], BF16, "wkv_b")
            qnT = sb(s2, [128, 4, NOWN], BF16, "qnT")
            KT = sb(s2, [128, TK], BF16, "KT")
            Vh = sb(s2, [128, 64, 128], BF16, "Vh")
            QN = sb(s2, [128, NOWN], BF16, "QN")
            QR = sb(s2, [128, NOWN], BF16, "QR")
            S.op("pool", lambda e: e.memset(QR.t[64:128, :], 0.0), writes=[V(None, [QR.b("z")])])
            sck = sb(s2, [128, 64], F32, "sck")
            sstmp = sb(s2, [128, 64], F32, "sstmp")
            sq2 = [sb(s2, [128, 512], BF16, f"sq2_{i}") for i in range(2)]
            sqr2 = sb(s2, [64, 512], BF16, "sqr2")
            sqq = sb(s2, [128, 512], BF16, "sqq")
            cq = sb(s2, [128, 512], F32, "cq")
            qg = sb(s2, [64, 512], F32, "qg")
            cs2 = sb(s2, [64, 512], F32, "cs2")
            sn2 = sb(s2, [64, 512], F32, "sn2")
            u1 = sb(s2, [64, 512], F32, "u1")
            u2 = sb(s2, [64, 512], F32, "u2")
            PT = [sb(s2, [128, 512], BF16, f"PT{i}") for i in range(4)]
            rl2 = sb(s2, [128, 512], F32, "rl2")
            tinyT = sb(s2, [128, 1], F32, "tinyT")
            memset("dve", tinyT.v(), 1e-30)
            oh = [sb(s2, [128, 512], BF16, f"oh{i}") for i in range(2)]
            with ExitStack() as s0:
                stage_w2 = [sb(s0, [128, 2048], F32, f"wst2_{i}") for i in range(2)]
                src_q = w_q_up.t.rearrange("(k p) c -> k p c", p=128)
                src_kv = w_kv_up.t.rearrange("(k p) c -> k p c", p=128)
                for k in range(4):
                    S.dma("sp", stage_w2[k % 2].v((slice(None), slice(0, 1536))), V(src_q[k], []))
                    cp("pool" if k % 2 else "dve", V(wq_b.t[:, k, :], [wq_b.b()]), stage_w2[k % 2].v((slice(None), slice(0, 1536))))
                for k in range(2):
                    S.dma("sp", stage_w2[k % 2].v(), V(src_kv[k], []))
                    cp("pool" if k % 2 else "dve", V(wkv_b.t[:, k, :], [wkv_b.b()]), stage_w2[k % 2].v())
                S.dma("sp", qnT.v(), qn_d.v())
                S.barrier()

            cst_f = [sb(s2, [128, 1408], F32, f"cst_f{i}") for i in range(2)]
            cst_b = [sb(s2, [128, 1408], BF16, f"cst_b{i}") for i in range(2)]

            loaded = [-1]

            def cast_load(i):
                kind, wsrc, wdst, a_, b_ = cast_jobs[i]
                cf = cst_f[i % 2]
                if kind == "gu":
                    S.dma("sp", cf.v(), V(wsrc.t[a_ * 128:(a_ + 1) * 128, b_ * 1408:(b_ + 1) * 1408], []))
                else:
                    S.dma("sp", cf.v((slice(None), slice(0, 1024))),
                          V(wsrc.t[a_ * 128:(a_ + 1) * 128, b_ * 1024:(b_ + 1) * 1024], []))
                loaded[0] = i

            def issue_casts(n):
                for _ in range(n):
                    if cast_pos[0] >= len(cast_jobs):
                        return
                    i = cast_pos[0]
                    kind, wsrc, wdst, a_, b_ = cast_jobs[i]
                    cast_pos[0] += 1
                    cf, cb = cst_f[i % 2], cst_b[i % 2]
                    if loaded[0] < i:
                        cast_load(i)
                    if i + 1 < len(cast_jobs):
                        cast_load(i + 1)
                    if kind == "gu":
                        k, q = a_, b_
                        cp("dve", cb.v(), cf.v())
                        S.dma("sp", V(wdst.t[q * 11:(q + 1) * 11, :, k, :].rearrange("j p c -> p j c"),
                                      [wdst.b(q * 11 + jj) for jj in range(11)]),
                              V(cb.t[:, :].rearrange("p (j c) -> p j c", c=128), [cb.b()]))
                    else:
                        j, hh = a_, b_
                        cp("dve", cb.v((slice(None), slice(0, 1024))), cf.v((slice(None), slice(0, 1024))))
                        S.dma("sp", V(wdst.t[j][:, hh * 1024:(hh + 1) * 1024], [wdst.b(j)]),
                              cb.v((slice(None), slice(0, 1024))))

            qchunks = [(0, 128)] + [(128 + 512 * i, 512) for i in range(8)]
            unit = [0]
            gunit = [0]
            for h in range(8):
                def k_gen():
                    for kc in range(16):
                        pz = pb[kc % 2]
                        for t in range(2):
                            mm(pz.v(), V(wkv_b.t[:, t, h * 256:h * 256 + 128], [wkv_b.b()]),
                               V(nT.t[:, t, kc * 512:(kc + 1) * 512], [nT.b()]), start=(t == 0), stop=(t == 1), signal=(t == 1))
                        act(V(KT.t[:, kc * 512:(kc + 1) * 512], [KT.b(kc)]), pz.v(), AF.Copy, scale=col(O_GKN))
                        sq = sq2[kc % 2]
                        act(sq.v(), pz.v(), AF.Square)
                        for b_ in range(4):
                            kb = kc * 4 + b_
                            mm(V(pb[2].t[:, kb:kb + 1], [pb[2].b()]), sq.v((slice(None), slice(b_ * 128, (b_ + 1) * 128))),
                               ones.v((slice(None), slice(0, 1))), signal=(b_ == 3))
                        yield
                    tt("dve", sstmp.v(), V(pb[2].t[:, 0:64], [pb[2].b()]), sskr.v(), ALU.add)
                    rsqrt_act(sstmp.v(), sstmp.v(), 1.0 / 192, epsT.v())
                    ts("dve", sck.v(), sstmp.v(), SC, ALU.mult)

                def v_gen():
                    for kg4 in range(16):
                        pz = pb[3 + kg4 % 2]
                        for b_ in range(4):
                            kb = kg4 * 4 + b_
                            for t in range(2):
                                mm(V(pz.t[:, b_ * 128:(b_ + 1) * 128], [pz.b()]), V(nT.t[:, t, kb * 128:(kb + 1) * 128], [nT.b()]),
                                   V(wkv_b.t[:, t, h * 256 + 128:h * 256 + 256], [wkv_b.b()]), start=(t == 0), stop=(t == 1),
                                   signal=(b_ == 3 and t == 1))
                        cp("dve" if kg4 % 2 else "act", V(Vh.t[:, kg4 * 4:(kg4 + 1) * 4, :], [Vh.b(kg4)]),
                           V(pz.t[:].rearrange("p (k c) -> p k c", c=128), [pz.b()]))
                        yield

                def q_gen():
                    for (qs, nq) in qchunks:
                        sl = (slice(None), slice(0, nq))
                        sl64 = (slice(0, 64), slice(0, nq))
                        S.dma("sp", cs2.v(sl64), V(cosT.t[:, OWN0 + qs:OWN0 + qs + nq], []))
                        S.dma("sp", sn2.v(sl64), V(sinT.t[:, OWN0 + qs:OWN0 + qs + nq], []))
                        pqn, pqr, pss = pb[5], pb[6], pb[7]
                        prot = pb[6]
                        for t in range(4):
                            mm(V(pqn.t[:, 0:nq], [pqn.b()]), V(wq_b.t[:, t, h * 192:h * 192 + 128], [wq_b.b()]),
                               V(qnT.t[:, t, qs:qs + nq], [qnT.b()]), start=(t == 0), stop=(t == 3), signal=(t == 3))
                        for t in range(4):
                            mm(V(pqr.t[0:64, 0:nq], [pqr.b()]), V(wq_b.t[:, t, h * 192 + 128:h * 192 + 192], [wq_b.b()]),
                               V(qnT.t[:, t, qs:qs + nq], [qnT.b()]), start=(t == 0), stop=(t == 3), signal=(t == 3))
                        yield
                        act(sqq.v(sl), V(pqn.t[:, 0:nq], [pqn.b()]), AF.Square)
                        act(sqr2.v(sl64), V(pqr.t[0:64, 0:nq], [pqr.b()]), AF.Square)
                        act(qg.v(sl64), V(pqr.t[0:64, 0:nq], [pqr.b()]), AF.Copy, scale=col(O_GQR, 1, 64))
                        yield
                        mm(V(pss.t[:, 0:nq], [pss.b()]), ones.v(), sqq.v(sl), start=True, stop=False, signal=False)
                        mm(V(pss.t[:, 0:nq], [pss.b()]), ones.v((slice(0, 64), slice(None))), sqr2.v(sl64), start=False, stop=True)
                        mm(V(prot.t[0:64, 0:nq], [prot.b()]), rotT.v(), qg.v(sl64))
                        yield
                        rsqrt_act(cq.v(sl), V(pss.t[:, 0:nq], [pss.b()]), 1.0 / 192, epsT.v())
                        tt("dve", u1.v(sl64), qg.v(sl64), cs2.v(sl64), ALU.mult)
                        tt("dve", u2.v(sl64), V(prot.t[0:64, 0:nq], [prot.b()]), sn2.v(sl64), ALU.mult)
                        yield
                        stt("dve", V(QN.t[:, qs:qs + nq], [QN.b(qs)]), V(pqn.t[:, 0:nq], [pqn.b()]), col(O_GQN), cq.v(sl),
                            ALU.mult, ALU.mult)
                        tt("dve", u1.v(sl64), u1.v(sl64), u2.v(sl64), ALU.add)
                        tt("dve", V(QR.t[0:64, qs:qs + nq], [QR.b(qs)]), u1.v(sl64), cq.v(sl64), ALU.mult)
                        yield

                gens = [k_gen(), v_gen(), q_gen()]
                while gens:
                    for gg in list(gens):
                        try:
                            next(gg)
                        except StopIteration:
                            gens.remove(gg)
                units = []
                for ci, (qs, nq) in enumerate(qchunks):
                    j0 = qs // 128
                    nqb = nq // 128
                    nfull = 31 + j0
                    order = [(0, None)] + [(31 + j0 + i, i) for i in range(nqb)] + [(kb, None) for kb in range(1, nfull)]
                    for oi, (kb, di) in enumerate(order):
                        units.append((ci, qs, nq, kb, di, oi == 0, oi == len(order) - 1))
                LA = 2
                for ui in range(len(units) + LA):
                    back = None
                    if ui >= LA:
                        uj = ui - LA
                        ci_b, qs_b, nq_b, kb_b, di_b, first_b, last_b = units[uj]
                        c0_b = 0 if di_b is None else di_b * 128
                        ptb = PT[uj % 4]
                        po_ = pb[4 + ci_b % 2]
                        pl_ = pb[6 + ci_b % 2]
                        cslb = (slice(None), slice(c0_b, nq_b))
                        S.prewait("pe", reads=[ptb.v(cslb), V(None, [Vh.b(kb_b // 4)])],
                                  writes=[V(None, [po_.b()]), V(None, [pl_.b()])])
                        back = True
                    if ui < len(units):
                        ci, qs, nq, kb, di, first, last = units[ui]
                        c0 = 0 if di is None else di * 128
                        ps_ = pb[ui % 4]
                        pt_ = PT[ui % 4]
                        csl = (slice(None), slice(c0, nq))
                        mm(V(ps_.t[:, c0:nq], [ps_.b()]), V(KT.t[:, kb * 128:(kb + 1) * 128], [KT.b(kb // 4)]),
                           V(QN.t[:, qs + c0:qs + nq], [QN.b(qs)]), start=True, stop=False, signal=False)
                        mm(V(ps_.t[:, c0:nq], [ps_.b()]), V(krT.t[:, kb * 128:(kb + 1) * 128], [krT.b()]),
                           V(QR.t[:, qs + c0:qs + nq], [QR.b(qs)]), start=False, stop=True)
                        act(pt_.v(csl), V(ps_.t[:, c0:nq], [ps_.b()]), AF.Exp,
                            scale=sck.v((slice(None), slice(kb, kb + 1))), bias=kbias.v((slice(None), slice(kb, kb + 1))))
                        if di is not None:
                            tt("dve", pt_.v((slice(None), slice(c0, c0 + 128))), pt_.v((slice(None), slice(c0, c0 + 128))),
                               tri.v(), ALU.mult)
                    gunit[0] += 1
                    if gunit[0] % 15 == 0:
                        issue_casts(1)
                    if back:
                        mm(V(po_.t[:, c0_b:nq_b], [po_.b()]), V(Vh.t[:, kb_b, :], [Vh.b(kb_b // 4)]), ptb.v(cslb),
                           start=first_b, stop=last_b, signal=False)
                        mm(V(pl_.t[:, c0_b:nq_b], [pl_.b()]), ones.v(), ptb.v(cslb), start=first_b, stop=last_b)
                        if last_b:
                            sl = (slice(None), slice(0, nq_b))
                            act(rl2.v(sl), V(pl_.t[:, 0:nq_b], [pl_.b()]), AF.Ln, bias=tinyT.v())
                            act(rl2.v(sl), rl2.v(sl), AF.Exp, scale=-1.0)
                            o_ = oh[ci_b % 2]
                            tt("dve", o_.v(sl), V(po_.t[:, 0:nq_b], [po_.b()]), rl2.v(sl), ALU.mult)
                            S.dma("sp", V(cat_d.t[:, 4 + h, qs_b:qs_b + nq_b], [cat_d.b()]), o_.v(sl))
                            ck(f"p2_h{h}_c{ci_b}")
                ck(f"p2_h{h}")
            issue_casts(1000)
            S.barrier()
        p12.close()

        with ExitStack() as s3:
            wo_sb = sb(s3, [128, 16, D], BF16, "wo_sb")
            uT = sb(s3, [128, NFT, 512], BF16, "uT")
            h2T = sb(s3, [128, 16, 512], BF16, "h2T")
            h2halo = sb(s3, [128, 16, 2], BF16, "h2halo")
            catc = [sb(s3, [128, 16, 128], BF16, f"catc{i}") for i in range(2)]
            wgs = [sb(s3, [128, 16, 128], BF16, f"wgs{i}") for i in range(2)]
            wus = [sb(s3, [128, 16, 128], BF16, f"wus{i}") for i in range(2)]
            wds = [sb(s3, [128, 4, 512], BF16, f"wds{i}") for i in range(3)]
            xt3 = sb(s3, [128, D], F32, "xt3")
            xb3 = sb(s3, [128, D], BF16, "xb3")
            ss3 = sb(s3, [128, 1], F32, "ss3")
            rs3 = sb(s3, [128, 1], F32, "rs3")
            gbuf = sb(s3, [128, 514], F32, "gbuf")
            acc = sb(s3, [128, 512], F32, "acc")
            sg = sb(s3, [128, 512], F32, "sg")
            ghist = sb(s3, [128, NFT, 2], F32, "ghist")
            outs = [sb(s3, [128, 512], F32, f"outs{i}") for i in range(2)]
            x1r = [sb(s3, [128, 512], F32, f"x1r{i}") for i in range(2)]
            with ExitStack() as s0:
                stage_w3 = [sb(s0, [128, 1024], F32, f"wst3_{i}") for i in range(2)]
                src_o = w_o.t.rearrange("(k p) c -> k p c", p=128)
                for k2 in range(32):
                    k, hh = k2 // 2, k2 % 2
                    S.dma("sp", stage_w3[k2 % 2].v(), V(src_o[k][:, hh * 1024:(hh + 1) * 1024], []))
                    cp("pool" if k2 % 2 else "dve", V(wo_sb.t[:, k, hh * 1024:(hh + 1) * 1024], [wo_sb.b((k, hh))]),
                       stage_w3[k2 % 2].v())
                S.barrier()

            x1r += [sb(s3, [128, 512], F32, f"x1r{i}") for i in (2, 3)]
            wds.append(sb(s3, [128, 4, 512], BF16, "wds3"))
            gcolb = V(cols.t[:, O_GFFN:O_GFFN + 16].unsqueeze(2).to_broadcast([128, 16, 128]), [cols.b()])
            gcolb8a = V(cols.t[:, O_GFFN:O_GFFN + 8].unsqueeze(2).to_broadcast([128, 8, 128]), [cols.b()])
            gcolb8b = V(cols.t[:, O_GFFN + 8:O_GFFN + 16].unsqueeze(2).to_broadcast([128, 8, 128]), [cols.b()])
            p3chunks = [(0, 128)] + [(128 + 512 * i, 512) for i in range(8)]
            wcnt = [0]
            blk3 = [0]
            for ci, (cs, n) in enumerate(p3chunks):
                nb_ = n // 128
                for b_ in range(nb_):
                    r0 = cs + b_ * 128
                    cc = catc[blk3[0] % 2]
                    blk3[0] += 1
                    S.dma("sp", cc.v(), V(cat_d.t[:, :, r0:r0 + 128], [cat_d.b()]))
                    S.dma("sp", xt3.v(), V(xk.t[OWN0 + r0:OWN0 + r0 + 128, :], []))
                    for dc in range(4):
                        pw = pb[2 + dc]
                        for f in range(16):
                            mm(pw.v(), V(cc.t[:, f, :], [cc.b()]), V(wo_sb.t[:, f, dc * 512:(dc + 1) * 512], [wo_sb.b((f, dc // 2))]),
                               start=(f == 0), stop=(f == 15), signal=(f == 15))
                        tt("dve", xt3.v((slice(None), slice(dc * 512, (dc + 1) * 512))), pw.v(),
                           xt3.v((slice(None), slice(dc * 512, (dc + 1) * 512))), ALU.add)
                    S.dma("sp", V(x1_d.t[r0:r0 + 128, :], [x1_d.b()]), xt3.v())
                    act(xb3.v(), xt3.v(), AF.Square, accum=ss3.v())
                    rsqrt_act(rs3.v(), ss3.v(), 1.0 / D, epsT.v())
                    ts("dve", xb3.v(), xt3.v(), rs3.v(), ALU.mult)
                    for k in range(16):
                        bank = pb[k // 8]
                        tr(V(pbf(k // 8)[:, k % 8, :], [bank.b()]), xb3.v((slice(None), slice(k * 128, (k + 1) * 128))),
                           idb.v(), signal=(k % 8 == 7))
                    c0 = b_ * 128
                    tt("dve", V(h2T.t[:, 0:8, c0:c0 + 128], [h2T.b(("a", c0))]), V(pbf(0), [pb[0].b()]), gcolb8a, ALU.mult)
                    tt("dve", V(h2T.t[:, 8:16, c0:c0 + 128], [h2T.b(("b", c0))]), V(pbf(1), [pb[1].b()]), gcolb8b, ALU.mult)
                if ci == 0:
                    cp("dve", h2halo.v(), V(h2T.t[:, :, 126:128], [h2T.b(("a", 0)), h2T.b(("b", 0))]))
                    ck("p3_c0")
                    continue

                def h2_rhs(k):
                    return V(h2T.t[:, k, :], [h2T.b((("a" if k < 8 else "b"), c0)) for c0 in range(0, 512, 128)])

                for j in range(NFT):
                    wg_ = wgs[j % 2]
                    wu_ = wus[j % 2]
                    S.dma("sp", wg_.v(), V(wg_b.t[j], [wg_b.b(j)]))
                    S.dma("sp", wu_.v(), V(wu_b.t[j], [wu_b.b(j)]))
                    if ci == 1:
                        ph = pb[6]
                        for k in range(16):
                            mm(V(ph.t[:, 0:2], [ph.b()]), V(wg_.t[:, k, :], [wg_.b()]), V(h2halo.t[:, k, :], [h2halo.b()]),
                               start=(k == 0), stop=(k == 15), signal=(k == 15))
                        act(V(ghist.t[:, j, :], [ghist.b(j)]), V(ph.t[:, 0:2], [ph.b()]), AF.Copy, scale=halo.v())
                    pg = pb[2 + j % 2]
                    pu = pb[4 + j % 2]
                    for k in range(16):
                        mm(pg.v(), V(wg_.t[:, k, :], [wg_.b()]), h2_rhs(k), start=(k == 0), stop=(k == 15), signal=(k == 15))
                    for k in range(16):
                        mm(pu.v(), V(wu_.t[:, k, :], [wu_.b()]), h2_rhs(k), start=(k == 0), stop=(k == 15), signal=(k == 15))
                    cp("pool", gbuf.v((slice(None), slice(0, 2))), V(ghist.t[:, j, :], [ghist.b(j)]))
                    cp("act", gbuf.v((slice(None), slice(2, 514))), pg.v())
                    act(acc.v(), pg.v(), AF.Identity, scale=col(O_CW + 2 * NFT + j), bias=col(O_CB + j))
                    cp("pool", V(ghist.t[:, j, :], [ghist.b(j)]), gbuf.v((slice(None), slice(512, 514))))
                    stt("dve", acc.v(), gbuf.v((slice(None), slice(1, 513))), col(O_CW + NFT + j), acc.v(), ALU.mult, ALU.add)
                    stt("dve", acc.v(), gbuf.v((slice(None), slice(0, 512))), col(O_CW + j), acc.v(), ALU.mult, ALU.add)
                    act(sg.v(), acc.v(), AF.Silu)
                    tt("dve", V(uT.t[:, j, :], [uT.b(j)]), sg.v(), pu.v(), ALU.mult)
                for dc in range(4):
                    for b_ in range(4):
                        r0 = cs + b_ * 128
                        S.dma("sp", x1r[b_].v(), V(x1_d.t[r0:r0 + 128, dc * 512:(dc + 1) * 512], [x1_d.b()]))
                    for jg in range(NFT // 4):
                        wd_ = wds[wcnt[0] % 4]
                        wcnt[0] += 1
                        S.dma("sp", wd_.v(), V(wd_b.t[jg * 4:(jg + 1) * 4, :, dc * 512:(dc + 1) * 512].rearrange("j p c -> p j c"),
                                               [wd_b.b(jg * 4 + i) for i in range(4)]))
                        for jj in range(4):
                            j = jg * 4 + jj
                            for b_ in range(4):
                                pd = pb[(0, 1, 6, 7)[b_]]
                                mm(pd.v(), V(uT.t[:, j, b_ * 128:(b_ + 1) * 128], [uT.b(j)]), V(wd_.t[:, jj, :], [wd_.b()]),
                                   start=(j == 0), stop=(j == NFT - 1), signal=(j == NFT - 1 or (jj == 3 and b_ == 3)))
                    for b_ in range(4):
                        pd = pb[(0, 1, 6, 7)[b_]]
                        r0 = cs + b_ * 128
                        xr = x1r[b_]
                        o_ = outs[b_ % 2]
                        tt("dve", o_.v(), pd.v(), xr.v(), ALU.add)
                        S.dma("act", V(out_t.t[r0 - 128:r0, dc * 512:(dc + 1) * 512], [out_t.b()]), o_.v())
                ck(f"p3_c{ci}")
            S.finish()


_NC_CACHE = {}


def _host_tables(core):
    half = 32
    inv_freq = (1.0 / (10000.0 ** (np.arange(half, dtype=np.float32) / np.float32(half)))).astype(np.float32)
    base = -4096 if core == 0 else 0
    pos = (np.arange(TK) + base).astype(np.float32)
    ang = pos[None, :] * inv_freq[:, None]
    cos = np.cos(ang).astype(np.float32)
    sin = np.sin(ang).astype(np.float32)
    cosT = np.concatenate([cos, cos], 0)
    sinT = np.concatenate([sin, sin], 0)
    kbias = np.zeros((128, 64), np.float32)
    if core == 0:
        kbias[:, :32] = -30000.0
    halo = np.full((128, 1), 0.0 if core == 0 else 1.0, np.float32)
    pinv = np.zeros((128, 4, P1N), np.float32)
    t = np.arange(P1N)
    for g, w in enumerate((2, 4, 8, 16)):
        cnt = np.minimum(t + 1, w) if core == 0 else np.full(P1N, w)
        pinv[:, g, :] = (1.0 / cnt.astype(np.float32))[None, :]
    return cosT, sinT, kbias, halo, pinv


def kernel(**inputs):
    x = np.asarray(inputs["x"], np.float32)
    mem = np.asarray(inputs["mem"], np.float32)
    if "nc" not in _NC_CACHE:
        _NC_CACHE["nc"] = build_nc()
    nc = _NC_CACHE["nc"]
    rot = np.zeros((64, 64), np.float32)
    for m in range(32):
        rot[m + 32, m] = -1.0
        rot[m, m + 32] = 1.0
    tri = np.triu(np.ones((128, 128), np.float32))
    shared = {
        "ident_f": np.eye(128, dtype=np.float32), "rotT": rot, "tri": tri,
    }
    for name in ("g_mix", "g_ffn", "g_mem", "g_q_lat", "g_kv_lat", "g_q_mla", "g_k_mla", "pool_scale", "g_q_x", "g_k_x",
                 "conv_w", "conv_b", "w_in", "w_q_up", "w_kv_up", "w_pool", "w_mem_kv", "w_o", "w_gate", "w_up", "w_down"):
        shared[name] = np.ascontiguousarray(np.asarray(inputs[name], np.float32)[0])
    tabs = [_host_tables(0), _host_tables(1)]
    in_maps = []
    for b in range(NB):
        for c in range(2):
            if c == 0:
                xk = np.concatenate([np.zeros((4096, D), np.float32), x[b, :4096]], 0)
            else:
                xk = x[b]
            cosT, sinT, kbias, halo, pinv = tabs[c]
            m = dict(shared)
            m.update({"xk": np.ascontiguousarray(xk), "mem": np.ascontiguousarray(mem[b]), "cosT": cosT, "sinT": sinT,
                      "kbias": kbias, "halo_flag": halo, "pool_inv": pinv})
            in_maps.append(m)
    res = run_bass_kernel_spmd(nc, in_maps, core_ids=list(range(8)))
    out = np.empty((NB, SEQ, D), np.float32)
    for b in range(NB):
        for c in range(2):
            out[b, c * 4096:(c + 1) * 4096] = np.asarray(res.results[b * 2 + c]["out"], np.float32)
    return out
```

```python
import math
from contextlib import ExitStack

import numpy as np
import concourse.bass as bass
import concourse.mybir as mybir
from concourse.bass_utils import run_bass_kernel_spmd

F32 = mybir.dt.float32
BF16 = mybir.dt.bfloat16
ALU = mybir.AluOpType
AF = mybir.ActivationFunctionType

D = 2048
SEQ = 8192
NB = 4
MEM = 256
IN_COLS = 1856
DFF = 5632
NFT = DFF // 128
TK = 8192
OWN0 = 3968
NOWN = 4224
EPS = 1e-6
P1N = 256
C_POOL, C_Q, C_KV, C_KR, C_MQ = 0, 512, 1024, 1280, 1344


class Buf:
    __slots__ = ("name", "w", "r", "multi")

    def __init__(self, name, multi=False):
        self.name = name
        self.w = {}
        self.r = {}
        self.multi = multi


class Ev:
    __slots__ = ("key", "val", "clock")

    def __init__(self, key, val, clock):
        self.key = key
        self.val = val
        self.clock = clock


class V:
    __slots__ = ("ap", "bufs")

    def __init__(self, ap, bufs):
        self.ap = ap
        self.bufs = bufs


class Tile:
    def __init__(self, t, name, multi=False):
        self.t = t
        self.name = name
        self.multi = multi
        self.bufs = {}

    def b(self, k=None):
        if k not in self.bufs:
            self.bufs[k] = Buf(f"{self.name}.{k}", self.multi)
        return self.bufs[k]

    def v(self, idx=None, b=None):
        ap = self.t[:] if idx is None else self.t[idx]
        if b == "*":
            bufs = list(self.bufs.values())
        elif isinstance(b, (list, tuple)):
            bufs = [self.b(x) for x in b]
        else:
            bufs = [self.b(b)]
        return V(ap, bufs)


class Sched:
    EPOCH = 8000
    NDMA = 40

    def __init__(self, nc, stack):
        self.nc = nc
        self.stack = stack
        self.eng = {"pe": nc.tensor, "act": nc.scalar, "dve": nc.vector,
                    "pool": nc.gpsimd, "sp": nc.sync}
        self.count = {e: 0 for e in self.eng}
        self.clock = {e: {} for e in self.eng}
        self.sems = {e: [] for e in self.eng}
        self.dma_sems = [stack.enter_context(nc.semaphore(f"dq{i}")) for i in range(self.NDMA)]
        self.dma_val = [0] * self.NDMA
        self.dma_last = [None] * self.NDMA
        self.dma_i = 0
        self.nwaits = 0
        self.ninst = 0
        self.pending = {e: False for e in self.eng}

    def _sem(self, e, cnt):
        ep = (cnt - 1) // self.EPOCH
        while len(self.sems[e]) <= ep:
            self.sems[e].append(self.stack.enter_context(
                self.nc.semaphore(f"s_{e}_{len(self.sems[e])}")))
        return self.sems[e][ep], (cnt - 1) % self.EPOCH + 1

    def _wait_ev(self, e, ev):
        eng = self.eng[e]
        if isinstance(ev.key, tuple):
            eng.wait_ge(self.dma_sems[ev.key[1]], ev.val)
        else:
            sem, v = self._sem(ev.key, ev.val)
            eng.wait_ge(sem, v)
        self.nwaits += 1

    def _wait(self, e, deps):
        clk = self.clock[e]
        best = {}
        for ev in deps:
            if ev is None or clk.get(ev.key, 0) >= ev.val:
                continue
            o = best.get(ev.key)
            if o is None or o.val < ev.val:
                best[ev.key] = ev
        for ev in sorted(best.values(), key=lambda x: -len(x.clock)):
            if clk.get(ev.key, 0) >= ev.val:
                continue
            self._wait_ev(e, ev)
            for k, v in ev.clock.items():
                if clk.get(k, 0) < v:
                    clk[k] = v
            clk[ev.key] = ev.val

    def _deps(self, e, reads, writes):
        deps = []
        same_raw = e in ("act", "dve", "pool")
        for b in reads:
            for ev in b.w.values():
                if ev.key == e and not same_raw:
                    continue
                deps.append(ev)
        for b in writes:
            if b.multi:
                continue
            for ev in b.w.values():
                if ev.key != e:
                    deps.append(ev)
            for ev in b.r.values():
                if ev.key != e:
                    deps.append(ev)
        return deps

    def _record(self, ev, reads, writes):
        for b in reads:
            o = b.r.get(ev.key)
            if o is None or o.val < ev.val:
                b.r[ev.key] = ev
        for b in writes:
            if b.multi:
                o = b.w.get(ev.key)
                if o is None or o.val < ev.val:
                    b.w[ev.key] = ev
            else:
                b.w = {ev.key: ev}
                b.r = {}

    def op(self, e, fn, reads=(), writes=(), signal=True):
        rb = [b for v in reads for b in v.bufs]
        wb = [b for v in writes for b in v.bufs]
        self._wait(e, self._deps(e, rb, wb))
        inst = fn(self.eng[e])
        self.ninst += 1
        if signal:
            self.count[e] += 1
            sem, _ = self._sem(e, self.count[e])
            inst.then_inc(sem, 1)
            ev = Ev(e, self.count[e], dict(self.clock[e]))
            self.pending[e] = False
        else:
            ev = Ev(e, self.count[e] + 1, {})
            self.pending[e] = True
        self._record(ev, rb, wb)
        return ev

    def dma(self, q, out, in_, **kw):
        rb = list(in_.bufs)
        wb = list(out.bufs)
        deps = self._deps(q, rb, wb)
        i = self.dma_i
        self.dma_i = (i + 1) % self.NDMA
        deps.append(self.dma_last[i])
        self._wait(q, deps)
        self.dma_val[i] += 16
        self.eng[q].dma_start(out=out.ap, in_=in_.ap, **kw).then_inc(self.dma_sems[i], 16)
        self.ninst += 1
        ev = Ev(("d", i), self.dma_val[i], dict(self.clock[q]))
        self.dma_last[i] = ev
        self._record(ev, rb, wb)
        return ev

    def prewait(self, e, reads=(), writes=()):
        rb = [b for v in reads for b in v.bufs]
        wb = [b for v in writes for b in v.bufs]
        self._wait(e, self._deps(e, rb, wb))

    def barrier(self):
        assert not any(self.pending.values()), self.pending
        evs = [Ev(e, self.count[e], {}) for e in ("pe", "act", "dve", "pool") if self.count[e] > 0]
        evs += [x for x in self.dma_last if x is not None]
        for e in self.eng:
            self._wait(e, [x for x in evs if x.key != e])

    def finish(self):
        assert not any(self.pending.values()), self.pending
        self._wait("sp", [x for x in self.dma_last if x is not None])


class _Stop(Exception):
    pass


def build_nc(debug=None, stop=None):
    nc = bass.Bass("TRN2", target_bir_lowering=False)

    def ck(name):
        if stop == name:
            raise _Stop()

    def din(name, shape, dt=F32):
        return Tile(nc.dram_tensor(name, shape, dt, kind="ExternalInput").ap(), name, multi=True)

    def dscr(name, shape, dt):
        return Tile(nc.dram_tensor(name, shape, dt).ap(), name, multi=True)

    xk = din("xk", [TK, D])
    mem = din("mem", [MEM, D])
    g_mix = din("g_mix", [D]); g_ffn = din("g_ffn", [D]); g_mem = din("g_mem", [D])
    g_q_lat = din("g_q_lat", [512]); g_kv_lat = din("g_kv_lat", [256])
    g_q_mla = din("g_q_mla", [192]); g_k_mla = din("g_k_mla", [192])
    pool_scale = din("pool_scale", [512]); g_q_x = din("g_q_x", [128]); g_k_x = din("g_k_x", [128])
    conv_w = din("conv_w", [3, DFF]); conv_b = din("conv_b", [DFF])
    w_in = din("w_in", [D, IN_COLS]); w_q_up = din("w_q_up", [512, 1536]); w_kv_up = din("w_kv_up", [256, 2048])
    w_pool = din("w_pool", [4, 128, 128]); w_mem_kv = din("w_mem_kv", [D, 1024]); w_o = din("w_o", [D, D])
    w_gate = din("w_gate", [D, DFF]); w_up = din("w_up", [D, DFF]); w_down = din("w_down", [DFF, D])
    cosT = din("cosT", [64, TK]); sinT = din("sinT", [64, TK])
    kbias_d = din("kbias", [128, 64]); halo_d = din("halo_flag", [128, 1]); pinv_d = din("pool_inv", [128, 4, P1N])
    ident_d = din("ident_f", [128, 128]); rot_d = din("rotT", [64, 64]); tri_d = din("tri", [128, 128])
    out_t = Tile(nc.dram_tensor("out", [4096, D], F32, kind="ExternalOutput").ap(), "out", multi=True)

    qn_d = dscr("qn_d", [128, 4, NOWN], BF16)
    cat_d = dscr("cat_d", [128, 16, NOWN], BF16)
    x1_d = dscr("x1_d", [NOWN, D], F32)
    wg_b = dscr("wg_b", [NFT, 128, 16, 128], BF16)
    wu_b = dscr("wu_b", [NFT, 128, 16, 128], BF16)
    wd_b = dscr("wd_b", [NFT, 128, D], BF16)
    dbg = None
    if debug:
        dbg = Tile(nc.dram_tensor("dbg", list(debug), F32, kind="ExternalOutput").ap(), "dbg", multi=True)

    with ExitStack() as st:
        S = Sched(nc, st)
        try:
            _program(nc, st, S, ck, locals())
        except _Stop:
            S.finish()
            st.pop_all()
        print("instructions", S.ninst, "waits", S.nwaits, {e: S.count[e] for e in S.count})
    return nc


def _program(nc, st, S, ck, G):
    globals_ = G
    xk = G["xk"]; mem = G["mem"]; g_mix = G["g_mix"]; g_ffn = G["g_ffn"]; g_mem = G["g_mem"]
    g_q_lat = G["g_q_lat"]; g_kv_lat = G["g_kv_lat"]; g_q_mla = G["g_q_mla"]; g_k_mla = G["g_k_mla"]
    pool_scale = G["pool_scale"]; g_q_x = G["g_q_x"]; g_k_x = G["g_k_x"]; conv_w = G["conv_w"]; conv_b = G["conv_b"]
    w_in = G["w_in"]; w_q_up = G["w_q_up"]; w_kv_up = G["w_kv_up"]; w_pool = G["w_pool"]; w_mem_kv = G["w_mem_kv"]
    w_o = G["w_o"]; w_gate = G["w_gate"]; w_up = G["w_up"]; w_down = G["w_down"]; cosT = G["cosT"]; sinT = G["sinT"]
    kbias_d = G["kbias_d"]; halo_d = G["halo_d"]; pinv_d = G["pinv_d"]; ident_d = G["ident_d"]; rot_d = G["rot_d"]
    tri_d = G["tri_d"]; out_t = G["out_t"]; qn_d = G["qn_d"]; cat_d = G["cat_d"]; x1_d = G["x1_d"]
    wg_b = G["wg_b"]; wu_b = G["wu_b"]; wd_b = G["wd_b"]; dbg = G["dbg"]; debug = G["debug"]
    if True:

        def sb(stack, shape, dt, name, multi=False):
            return Tile(stack.enter_context(nc.sbuf_tensor("sb_" + name, shape, dt)), name, multi)

        def mm(o, l, r, start=True, stop=True, signal=True):
            return S.op("pe", lambda e: e.matmul(o.ap, lhsT=l.ap, rhs=r.ap, start=start, stop=stop),
                        reads=[l, r], writes=[o], signal=signal)

        def tr(o, i, ident, signal=True):
            return S.op("pe", lambda e: e.transpose(out=o.ap, in_=i.ap, identity=ident.ap),
                        reads=[i, ident], writes=[o], signal=signal)

        def act(o, i, func, scale=1.0, bias=None, accum=None):
            rd = [i]
            kw = {}
            if isinstance(scale, V):
                rd.append(scale); kw["scale"] = scale.ap
            else:
                kw["scale"] = scale
            if bias is not None:
                if isinstance(bias, V):
                    rd.append(bias); kw["bias"] = bias.ap
                else:
                    kw["bias"] = bias
            wr = [o]
            if accum is not None:
                wr.append(accum); kw["accum_out"] = accum.ap
            return S.op("act", lambda e: e.activation(out=o.ap, in_=i.ap, func=func, **kw), reads=rd, writes=wr)

        def tt(eng, o, a, b_, op):
            return S.op(eng, lambda e: e.tensor_tensor(out=o.ap, in0=a.ap, in1=b_.ap, op=op), reads=[a, b_], writes=[o])

        def ts(eng, o, a, s1, op0, s2=None, op1=None):
            rd = [a]
            if isinstance(s1, V):
                rd.append(s1); s1a = s1.ap
            else:
                s1a = s1
            if isinstance(s2, V):
                rd.append(s2); s2a = s2.ap
            else:
                s2a = s2
            kw = {} if op1 is None else {"op1": op1}
            return S.op(eng, lambda e: e.tensor_scalar(out=o.ap, in0=a.ap, scalar1=s1a, scalar2=s2a, op0=op0, **kw),
                        reads=rd, writes=[o])

        def stt(eng, o, a, s, b_, op0, op1):
            rd = [a, b_]
            if isinstance(s, V):
                rd.append(s); sa = s.ap
            else:
                sa = s
            return S.op(eng, lambda e: e.scalar_tensor_tensor(out=o.ap, in0=a.ap, scalar=sa, in1=b_.ap, op0=op0, op1=op1),
                        reads=rd, writes=[o])

        def cp(eng, o, i):
            if eng == "act":
                return S.op("act", lambda e: e.copy(out=o.ap, in_=i.ap), reads=[i], writes=[o])
            return S.op(eng, lambda e: e.tensor_copy(out=o.ap, in_=i.ap), reads=[i], writes=[o])

        def memset(eng, o, val):
            return S.op(eng, lambda e: e.memset(o.ap, val), writes=[o])

        def recip(o, i):
            return S.op("dve", lambda e: e.reciprocal(out=o.ap, in_=i.ap), reads=[i], writes=[o])

        def rsqrt_act(o, i, inv_n, eps_v):
            act(o, i, AF.Ln, scale=inv_n, bias=eps_v)
            act(o, o, AF.Exp, scale=-0.5)

        pb = [Tile(st.enter_context(nc.psum_tensor(f"pb{i}", [128, 512], F32)), f"pb{i}") for i in range(8)]

        def pbf(i):
            return pb[i].t[:].bitcast(BF16).rearrange("p (k c) -> p k c", c=128)

        idf = sb(st, [128, 128], F32, "idf")
        idb = sb(st, [128, 128], BF16, "idb")
        ones = sb(st, [128, 128], BF16, "ones")
        rotT = sb(st, [64, 64], F32, "rotT")
        tri = sb(st, [128, 128], BF16, "tri")
        epsT = sb(st, [128, 1], F32, "epsT")
        kbias = sb(st, [128, 64], F32, "kbias")
        halo = sb(st, [128, 1], F32, "halo")
        NCOL = 16 * 3 + 4 + 2 + 4 + 4 + 4 + 3 * NFT + NFT
        cols = sb(st, [128, NCOL], F32, "cols")
        O_GMIX, O_GFFN, O_GMEM = 0, 16, 32
        O_GQL, O_GKVL = 48, 52
        O_GQN, O_GQR, O_GKN, O_GKR = 54, 55, 56, 57
        O_PS = 58
        O_GQX, O_GKX = 62, 63
        O_CW = 66
        O_CB = 66 + 3 * NFT

        def col(o, n=1, p=128):
            return cols.v((slice(0, p), slice(o, o + n)))

        S.dma("sp", idf.v(), ident_d.v())
        cp("dve", idb.v(), idf.v())
        memset("dve", ones.v(), 1.0)
        memset("dve", epsT.v(), EPS)
        S.dma("sp", rotT.v(), rot_d.v())
        S.dma("sp", kbias.v(), kbias_d.v())
        S.dma("sp", halo.v(), halo_d.v())

        with ExitStack() as s0:
            stg = sb(s0, [128, 128], F32, "stg")
            S.dma("sp", stg.v(), tri_d.v())
            cp("dve", tri.v(), stg.v())

            def vec_cols(vec_ap, r, w, off):
                S.dma("sp", stg.v((slice(0, r), slice(0, w))), V(vec_ap, []))
                tr(V(pb[0].t[0:w, 0:r], [pb[0].b()]), stg.v((slice(0, r), slice(0, w))), idf.v((slice(0, r), slice(0, r))))
                cp("dve", cols.v((slice(0, w), slice(off, off + r))), V(pb[0].t[0:w, 0:r], [pb[0].b()]))

            def r128(t):
                return t.t.rearrange("(r p) -> r p", p=128)

            vec_cols(r128(g_mix), 16, 128, O_GMIX)
            vec_cols(r128(g_ffn), 16, 128, O_GFFN)
            vec_cols(r128(g_mem), 16, 128, O_GMEM)
            vec_cols(r128(g_q_lat), 4, 128, O_GQL)
            vec_cols(r128(g_kv_lat), 2, 128, O_GKVL)
            vec_cols(g_q_mla.t[0:128].rearrange("(r p) -> r p", p=128), 1, 128, O_GQN)
            vec_cols(g_q_mla.t[128:192].rearrange("(r p) -> r p", p=64), 1, 64, O_GQR)
            vec_cols(g_k_mla.t[0:128].rearrange("(r p) -> r p", p=128), 1, 128, O_GKN)
            vec_cols(g_k_mla.t[128:192].rearrange("(r p) -> r p", p=64), 1, 64, O_GKR)
            vec_cols(r128(pool_scale), 4, 128, O_PS)
            vec_cols(r128(g_q_x), 1, 128, O_GQX)
            vec_cols(r128(g_k_x), 1, 128, O_GKX)
            for j in range(3):
                vec_cols(conv_w.t[j].rearrange("(r p) -> r p", p=128), NFT, 128, O_CW + j * NFT)
            vec_cols(r128(conv_b), NFT, 128, O_CB)
            S.barrier()
        ck("p0")

        p12 = st.enter_context(ExitStack())
        nT = sb(p12, [128, 2, TK], BF16, "nT", multi=True)
        krT = sb(p12, [128, TK], BF16, "krT", multi=True)
        S.op("pool", lambda e: e.memset(krT.t[64:128, :], 0.0), writes=[V(None, [krT.b()])])
        sskr = sb(p12, [128, 64], F32, "sskr", multi=True)

        cast_jobs = []
        for wsrc, wdst in ((w_gate, wg_b), (w_up, wu_b)):
            for k in range(16):
                for q in range(4):
                    cast_jobs.append(("gu", wsrc, wdst, k, q))
        for j in range(NFT):
            for hh in range(2):
                cast_jobs.append(("d", w_down, wd_b, j, hh))
        cast_pos = [0]

        with ExitStack() as s1:
            winb = sb(s1, [128, 16, IN_COLS], BF16, "winb")
            xts = [sb(s1, [128, D], F32, f"xt{i}") for i in range(2)]
            xb = sb(s1, [128, D], BF16, "xb")
            ssx = sb(s1, [128, 1], F32, "ssx")
            rsx = sb(s1, [128, 1], F32, "rsx")
            hT = sb(s1, [128, 16, P1N], BF16, "hT")
            zts = [sb(s1, [128, P1N], F32, f"zt{i}") for i in range(4)]
            sqs = [sb(s1, [128, P1N], BF16, f"sq{i}") for i in range(2)]
            rss = [sb(s1, [128, P1N], F32, f"rs{i}") for i in range(2)]
            pbufs = [sb(s1, [128, 16 + P1N], F32, f"pbuf{i}") for i in range(4)]
            phist = [sb(s1, [128, 16], F32, f"phist{i}") for i in range(4)]
            ptmp = [sb(s1, [128, 16 + P1N], F32, f"ptmp{i}") for i in range(2)]
            dbf = [sb(s1, [128, P1N], BF16, f"dbf{i}") for i in range(2)]
            ycat = [sb(s1, [128, P1N], BF16, f"ycat{i}") for i in range(2)]
            ycx = [sb(s1, [128, P1N], BF16, f"ycx{i}") for i in range(2)]
            qn = sb(s1, [128, 4, P1N], BF16, "qn")
            KxT = sb(s1, [128, 4, MEM], BF16, "KxT")
            Vx = sb(s1, [128, 2, 512], BF16, "Vx")
            qx = sb(s1, [128, P1N], BF16, "qx")
            pexp = [sb(s1, [128, P1N], BF16, f"pexp{i}") for i in range(2)]
            rl1 = sb(s1, [128, P1N], F32, "rl1")
            sqr1 = sb(s1, [64, P1N], BF16, "sqr1")
            kg = sb(s1, [64, P1N], F32, "kg")
            cs_t = sb(s1, [64, P1N], F32, "cs_t")
            sn_t = sb(s1, [64, P1N], F32, "sn_t")
            t1 = sb(s1, [64, P1N], F32, "t1")
            t2 = sb(s1, [64, P1N], F32, "t2")
            pinv = sb(s1, [128, 4, P1N], F32, "pinv")
            wpool = sb(s1, [128, 4, 128], BF16, "wpool")
            S.dma("sp", pinv.v(), pinv_d.v())
            for g in range(4):
                memset("pool", phist[g].v(), 0.0)

            zi = [0]

            def next_zt():
                zi[0] += 1
                return zts[zi[0] % 4]

            pz_i = [0]

            def next_pz():
                pz_i[0] += 1
                return pb[2 + pz_i[0] % 4]

            xbs = [xb]
            hTs = [hT]
            curb = {"hT": hT, "xb": xb}

            def norm_pre(src_tile, row0, xt, xb):
                S.dma("sp", xt.v(), V(src_tile.t[row0:row0 + 128, :], []))
                act(xb.v(), xt.v(), AF.Square, accum=ssx.v())
                rsqrt_act(rsx.v(), ssx.v(), 1.0 / D, epsT.v())
                ts("dve", xb.v(), xt.v(), rsx.v(), ALU.mult)

            def norm_post(xb):
                for k in range(16):
                    bank = pb[k // 8]
                    tr(V(pbf(k // 8)[:, k % 8, :], [bank.b()]), xb.v((slice(None), slice(k * 128, (k + 1) * 128))),
                       idb.v(), signal=(k % 8 == 7))

            def norm_block(src_tile, row0, xt, gain_none=True):
                norm_pre(src_tile, row0, xt, curb["xb"])
                norm_post(curb["xb"])

            def evac_hT(dst, c0):
                cp("dve", V(dst.t[:, 0:8, c0:c0 + 128], [dst.b(("a", c0))]), V(pbf(0), [pb[0].b()]))
                cp("act", V(dst.t[:, 8:16, c0:c0 + 128], [dst.b(("b", c0))]), V(pbf(1), [pb[1].b()]))

            def hT_rhs(k, n):
                hT = curb["hT"]
                return V(hT.t[:, k, 0:n], [hT.b((("a" if k < 8 else "b"), c0)) for c0 in range(0, n, 128)])

            def proj(c0, wd, n, evac_eng):
                pz = next_pz()
                for k in range(16):
                    mm(V(pz.t[0:wd, 0:n], [pz.b()]), V(winb.t[:, k, c0:c0 + wd], [winb.b(k)]), hT_rhs(k, n),
                       start=(k == 0), stop=(k == 15), signal=(k == 15))
                z = next_zt()
                cp(evac_eng, z.v((slice(0, wd), slice(0, n))), V(pz.t[0:wd, 0:n], [pz.b()]))
                return z

            def stage_weight(s_, dst_fn, src_rows, ncols, k_tiles, gain_off, tag):
                stgs = [sb(s_, [128, ncols], F32, f"wst_{tag}{i}") for i in range(2)]
                for k in range(k_tiles):
                    stg_ = stgs[k % 2]
                    S.dma("sp", stg_.v(), V(src_rows[k], []))
                    if gain_off is None:
                        cp("dve", dst_fn(k), stg_.v())
                    else:
                        ts("dve", dst_fn(k), stg_.v(), col(gain_off + k), ALU.mult)

            with ExitStack() as s0:
                stage_weight(s0, lambda k: V(winb.t[:, k, 0:1024], [winb.b(k)]),
                             w_mem_kv.t.rearrange("(k p) c -> k p c", p=128), 1024, 16, O_GMEM, "m")
                stgp = sb(s0, [128, 4, 128], F32, "wst_p")
                S.dma("sp", stgp.v(), V(w_pool.t.rearrange("g c d -> c g d"), []))
                cp("dve", wpool.v(), stgp.v())
                for mb in range(2):
                    norm_block(mem, mb * 128, xts[mb])
                    evac_hT(hT, mb * 128)
                for h in range(4):
                    z = proj(h * 128, 128, MEM, "act")
                    sq = sqs[h % 2]
                    act(sq.v(), z.v(), AF.Square)
                    mm(V(pb[6].t[:, 0:MEM], [pb[6].b()]), ones.v(), sq.v())
                    rs = rss[h % 2]
                    rsqrt_act(rs.v(), V(pb[6].t[:, 0:MEM], [pb[6].b()]), 1.0 / 128, epsT.v())
                    stt("dve", V(KxT.t[:, h, :], [KxT.b(h)]), z.v(), col(O_GKX), rs.v(), ALU.mult, ALU.mult)
                for mb in range(2):
                    pz = next_pz()
                    for k in range(16):
                        mm(V(pz.t[:, 0:512], [pz.b()]),
                           V(hT.t[:, k, mb * 128:(mb + 1) * 128], [hT.b((("a" if k < 8 else "b"), mb * 128))]),
                           V(winb.t[:, k, 512:1024], [winb.b(k)]), start=(k == 0), stop=(k == 15), signal=(k == 15))
                    cp("act", V(Vx.t[:, mb, :], [Vx.b(mb)]), V(pz.t[:, 0:512], [pz.b()]))
                stage_weight(s0, lambda k: V(winb.t[:, k, :], [winb.b(k)]),
                             w_in.t.rearrange("(k p) c -> k p c", p=128), IN_COLS, 16, O_GMIX, "i")
                S.barrier()
            ck("mem")
            hTs.append(sb(s1, [128, 16, P1N], BF16, "hT2"))
            xbs.append(sb(s1, [128, D], BF16, "xb2"))

            chunks = [(0, 128)] + [(128 + P1N * i, P1N) for i in range(15)] + [(OWN0, 128)] + \
                     [(OWN0 + 128 + P1N * i, P1N) for i in range(16)]
            assert chunks[15][0] + chunks[15][1] == OWN0 and chunks[-1][0] + chunks[-1][1] == TK
            blk_i = 0
            for ci, (s, n) in enumerate(chunks):
                own = s >= OWN0
                os_ = s - OWN0
                first_real = (s == OWN0 + 128)
                nb_ = n // 128
                curb["hT"] = hTs[ci % 2]
                if ci == 0:
                    for b_ in range(nb_):
                        norm_pre(xk, s + b_ * 128, xts[b_], xbs[b_])
                    for b_ in range(nb_):
                        norm_post(xbs[b_])
                        evac_hT(hTs[0], b_ * 128)
                if ci + 1 < len(chunks):
                    s_n, n_n = chunks[ci + 1]
                    for b_ in range(n_n // 128):
                        norm_pre(xk, s_n + b_ * 128, xts[b_], xbs[b_])
                S.dma("sp", cs_t.v((slice(None), slice(0, n))), V(cosT.t[:, s:s + n], []))
                S.dma("sp", sn_t.v((slice(None), slice(0, n))), V(sinT.t[:, s:s + n], []))
                sl = (slice(None), slice(0, n))
                sl64 = (slice(0, 64), slice(0, n))

                def kv_gen():
                    zkv = []
                    for t in range(2):
                        zkv.append(proj(C_KV + t * 128, 128, n, "act" if t == 0 else "dve"))
                        yield
                    for t in range(2):
                        act(sqs[t].v(sl), zkv[t].v(sl), AF.Square)
                        mm(V(pb[6].t[:, 0:n], [pb[6].b()]), ones.v(), sqs[t].v(sl), start=(t == 0), stop=(t == 1),
                           signal=(t == 1))
                    yield
                    rsqrt_act(rss[0].v(sl), V(pb[6].t[:, 0:n], [pb[6].b()]), 1.0 / 256, epsT.v())
                    yield
                    for t in range(2):
                        stt("dve", V(nT.t[:, t, s:s + n], [nT.b()]), zkv[t].v(sl), col(O_GKVL + t), rss[0].v(sl),
                            ALU.mult, ALU.mult)

                def kr_gen():
                    zkr = proj(C_KR, 64, n, "act")
                    yield
                    act(sqr1.v(sl64), zkr.v(sl64), AF.Square)
                    for b_ in range(nb_):
                        mm(V(pb[7].t[:, b_:b_ + 1], [pb[7].b()]), sqr1.v((slice(0, 64), slice(b_ * 128, (b_ + 1) * 128))),
                           ones.v((slice(0, 64), slice(0, 1))), signal=(b_ == nb_ - 1))
                    cp("dve", V(sskr.t[:, s // 128:s // 128 + nb_], [sskr.b()]), V(pb[7].t[:, 0:nb_], [pb[7].b()]))
                    ts("dve", kg.v(sl64), zkr.v(sl64), col(O_GKR, 1, 64), ALU.mult)
                    yield
                    mm(V(pb[7].t[0:64, 0:n], [pb[7].b()]), rotT.v(), kg.v(sl64))
                    tt("dve", t1.v(sl64), kg.v(sl64), cs_t.v(sl64), ALU.mult)
                    yield
                    tt("dve", t2.v(sl64), V(pb[7].t[0:64, 0:n], [pb[7].b()]), sn_t.v(sl64), ALU.mult)
                    tt("dve", V(krT.t[0:64, s:s + n], [krT.b()]), t1.v(sl64), t2.v(sl64), ALU.add)

                gens = [kv_gen(), kr_gen()]
                while gens:
                    for gg in list(gens):
                        try:
                            next(gg)
                        except StopIteration:
                            gens.remove(gg)
                if own:
                    def pool_gen(g):
                        w = 2 << g
                        pz = next_pz()
                        for k in range(16):
                            mm(V(pz.t[:, 0:n], [pz.b()]), V(winb.t[:, k, C_POOL + g * 128:C_POOL + (g + 1) * 128], [winb.b(k)]),
                               hT_rhs(k, n), start=(k == 0), stop=(k == 15), signal=(k == 15))
                        yield
                        pbuf = pbufs[g]
                        L = 16 + n
                        cp("pool", pbuf.v((slice(None), slice(0, 16))), phist[g].v())
                        cp("act", pbuf.v((slice(None), slice(16, L))), V(pz.t[:, 0:n], [pz.b()]))
                        cp("pool", phist[g].v(), pbuf.v((slice(None), slice(n, L))))
                        yield
                        cur = pbuf
                        sh = 1
                        lo = 0
                        for lvl in range(g + 1):
                            lo2 = lo + sh
                            dst = ptmp[lvl % 2]
                            tt("pool", dst.v((slice(None), slice(lo2, L))), cur.v((slice(None), slice(lo2, L))),
                               cur.v((slice(None), slice(lo, L - sh))), ALU.add)
                            cur = dst
                            lo = lo2
                            sh *= 2
                            yield
                        d_ = dbf[g % 2]
                        if first_real:
                            tt("dve", cur.v((slice(None), slice(16, L))), cur.v((slice(None), slice(16, L))),
                               V(pinv.t[:, g, 0:n], [pinv.b()]), ALU.mult)
                            tt("dve", d_.v(sl), cur.v((slice(None), slice(16, L))), pbuf.v((slice(None), slice(16, L))),
                               ALU.subtract)
                        else:
                            stt("dve", d_.v(sl), cur.v((slice(None), slice(16, L))), 1.0 / w,
                                pbuf.v((slice(None), slice(16, L))), ALU.mult, ALU.subtract)
                        yield
                        pz2 = next_pz()
                        mm(V(pz2.t[:, 0:n], [pz2.b()]), V(wpool.t[:, g, :], [wpool.b()]), d_.v(sl))
                        yield
                        yc = ycat[g % 2]
                        act(yc.v(sl), V(pz2.t[:, 0:n], [pz2.b()]), AF.Copy, scale=col(O_PS + g))
                        S.dma("sp", V(cat_d.t[:, g, os_:os_ + n], [cat_d.b()]), yc.v(sl))

                    ck(f"p1_c{ci}_pool")
                    zq = []
                    for t in range(4):
                        z = proj(C_Q + t * 128, 128, n, "act" if t % 2 == 0 else "dve")
                        zq.append(z)
                        act(sqs[t % 2].v(sl), z.v(sl), AF.Square)
                        mm(V(pb[6].t[:, 0:n], [pb[6].b()]), ones.v(), sqs[t % 2].v(sl), start=(t == 0), stop=(t == 3),
                           signal=(t == 3))
                    rsqrt_act(rss[1].v(sl), V(pb[6].t[:, 0:n], [pb[6].b()]), 1.0 / 512, epsT.v())
                    for t in range(4):
                        stt("dve", V(qn.t[:, t, 0:n], [qn.b()]), zq[t].v(sl), col(O_GQL + t), rss[1].v(sl),
                            ALU.mult, ALU.mult)
                    S.dma("sp", V(qn_d.t[:, :, os_:os_ + n], [qn_d.b()]), V(qn.t[:, :, 0:n], [qn.b()]))
                    ck(f"p1_c{ci}_q")
                    def xattn_gen(h):
                        z = proj(C_MQ + h * 128, 128, n, "act")
                        yield
                        act(sqs[h % 2].v(sl), z.v(sl), AF.Square)
                        mm(V(pb[6].t[:, 0:n], [pb[6].b()]), ones.v(), sqs[h % 2].v(sl))
                        yield
                        rs = rss[h % 2]
                        rsqrt_act(rs.v(sl), V(pb[6].t[:, 0:n], [pb[6].b()]), 1.0 / 128, epsT.v())
                        yield
                        stt("dve", qx.v(sl), z.v(sl), col(O_GQX), rs.v(sl), ALU.mult, ALU.mult)
                        yield
                        for mb in range(2):
                            pz = next_pz()
                            mm(V(pz.t[:, 0:n], [pz.b()]), V(KxT.t[:, h, mb * 128:(mb + 1) * 128], [KxT.b(h)]), qx.v(sl))
                            act(pexp[mb].v(sl), V(pz.t[:, 0:n], [pz.b()]), AF.Exp, scale=1.0 / math.sqrt(128.0))
                        yield
                        po_ = next_pz()
                        for mb in range(2):
                            mm(V(po_.t[:, 0:n], [po_.b()]), V(Vx.t[:, mb, h * 128:(h + 1) * 128], [Vx.b(mb)]), pexp[mb].v(sl),
                               start=(mb == 0), stop=(mb == 1), signal=(mb == 1))
                        for mb in range(2):
                            mm(V(pb[6].t[:, 0:n], [pb[6].b()]), ones.v(), pexp[mb].v(sl),
                               start=(mb == 0), stop=(mb == 1), signal=(mb == 1))
                        yield
                        recip(rl1.v(sl), V(pb[6].t[:, 0:n], [pb[6].b()]))
                        yield
                        yc = ycx[h % 2]
                        tt("dve", yc.v(sl), V(po_.t[:, 0:n], [po_.b()]), rl1.v(sl), ALU.mult)
                        S.dma("sp", V(cat_d.t[:, 12 + h, os_:os_ + n], [cat_d.b()]), yc.v(sl))

                    for g in range(4):
                        gens = [pool_gen(g), xattn_gen(g)]
                        while gens:
                            for gg in list(gens):
                                try:
                                    next(gg)
                                except StopIteration:
                                    gens.remove(gg)
                if ci + 1 < len(chunks):
                    s_n, n_n = chunks[ci + 1]
                    for b_ in range(n_n // 128):
                        norm_post(xbs[b_])
                        evac_hT(hTs[(ci + 1) % 2], b_ * 128)
                ck(f"p1_c{ci}")
            ck("p1")
            if debug == "p1":
                pass
            S.barrier()

        SC = 1.0 / math.sqrt(192.0)
        with ExitStack() as s2:
            wq_b = sb(s2, [128, 4, 1536], BF16, "wq_b")
            wkv_b = sb(s2, [128, 2, 2048], BF16, "wkv_b")
            qnT = sb(s2, [128, 4, NOWN], BF16, "qnT")
            KT = sb(s2, [128, TK], BF16, "KT")
            Vh = sb(s2, [128, 64, 128], BF16, "Vh")
            QN = sb(s2, [128, NOWN], BF16, "QN")
            QR = sb(s2, [128, NOWN], BF16, "QR")
            S.op("pool", lambda e: e.memset(QR.t[64:128, :], 0.0), writes=[V(None, [QR.b("z")])])
            sck = sb(s2, [128, 64], F32, "sck")
            sstmp = sb(s2, [128, 64], F32, "sstmp")
            sq2 = [sb(s2, [128, 512], BF16, f"sq2_{i}") for i in range(2)]
            sqr2 = sb(s2, [64, 512], BF16, "sqr2")
            sqq = sb(s2, [128, 512], BF16, "sqq")
            cq = sb(s2, [128, 512], F32, "cq")
            qg = sb(s2, [64, 512], F32, "qg")
            cs2 = sb(s2, [64, 512], F32, "cs2")
            sn2 = sb(s2, [64, 512], F32, "sn2")
            u1 = sb(s2, [64, 512], F32, "u1")
            u2 = sb(s2, [64, 512], F32, "u2")
            PT = [sb(s2, [128, 512], BF16, f"PT{i}") for i in range(4)]
            rl2 = sb(s2, [128, 512], F32, "rl2")
            tinyT = sb(s2, [128, 1], F32, "tinyT")
            memset("dve", tinyT.v(), 1e-30)
            oh = [sb(s2, [128, 512], BF16, f"oh{i}") for i in range(2)]
            with ExitStack() as s0:
                stage_w2 = [sb(s0, [128, 2048], F32, f"wst2_{i}") for i in range(2)]
                src_q = w_q_up.t.rearrange("(k p) c -> k p c", p=128)
                src_kv = w_kv_up.t.rearrange("(k p) c -> k p c", p=128)
                for k in range(4):
                    S.dma("sp", stage_w2[k % 2].v((slice(None), slice(0, 1536))), V(src_q[k], []))
                    cp("pool" if k % 2 else "dve", V(wq_b.t[:, k, :], [wq_b.b()]), stage_w2[k % 2].v((slice(None), slice(0, 1536))))
                for k in range(2):
                    S.dma("sp", stage_w2[k % 2].v(), V(src_kv[k], []))
                    cp("pool" if k % 2 else "dve", V(wkv_b.t[:, k, :], [wkv_b.b()]), stage_w2[k % 2].v())
                S.dma("sp", qnT.v(), qn_d.v())
                S.barrier()

            cst_f = [sb(s2, [128, 1408], F32, f"cst_f{i}") for i in range(2)]
            cst_b = [sb(s2, [128, 1408], BF16, f"cst_b{i}") for i in range(2)]

            loaded = [-1]

            def cast_load(i):
                kind, wsrc, wdst, a_, b_ = cast_jobs[i]
                cf = cst_f[i % 2]
                if kind == "gu":
                    S.dma("sp", cf.v(), V(wsrc.t[a_ * 128:(a_ + 1) * 128, b_ * 1408:(b_ + 1) * 1408], []))
                else:
                    S.dma("sp", cf.v((slice(None), slice(0, 1024))),
                          V(wsrc.t[a_ * 128:(a_ + 1) * 128, b_ * 1024:(b_ + 1) * 1024], []))
                loaded[0] = i

            def issue_casts(n):
                for _ in range(n):
                    if cast_pos[0] >= len(cast_jobs):
                        return
                    i = cast_pos[0]
                    kind, wsrc, wdst, a_, b_ = cast_jobs[i]
                    cast_pos[0] += 1
                    cf, cb = cst_f[i % 2], cst_b[i % 2]
                    if loaded[0] < i:
                        cast_load(i)
                    if i + 1 < len(cast_jobs):
                        cast_load(i + 1)
                    if kind == "gu":
                        k, q = a_, b_
                        cp("dve", cb.v(), cf.v())
                        S.dma("sp", V(wdst.t[q * 11:(q + 1) * 11, :, k, :].rearrange("j p c -> p j c"),
                                      [wdst.b(q * 11 + jj) for jj in range(11)]),
                              V(cb.t[:, :].rearrange("p (j c) -> p j c", c=128), [cb.b()]))
                    else:
                        j, hh = a_, b_
                        cp("dve", cb.v((slice(None), slice(0, 1024))), cf.v((slice(None), slice(0, 1024))))
                        S.dma("sp", V(wdst.t[j][:, hh * 1024:(hh + 1) * 1024], [wdst.b(j)]),
                              cb.v((slice(None), slice(0, 1024))))

            qchunks = [(0, 128)] + [(128 + 512 * i, 512) for i in range(8)]
            unit = [0]
            gunit = [0]
            for h in range(8):
                def k_gen():
                    for kc in range(16):
                        pz = pb[kc % 2]
                        for t in range(2):
                            mm(pz.v(), V(wkv_b.t[:, t, h * 256:h * 256 + 128], [wkv_b.b()]),
                               V(nT.t[:, t, kc * 512:(kc + 1) * 512], [nT.b()]), start=(t == 0), stop=(t == 1), signal=(t == 1))
                        act(V(KT.t[:, kc * 512:(kc + 1) * 512], [KT.b(kc)]), pz.v(), AF.Copy, scale=col(O_GKN))
                        sq = sq2[kc % 2]
                        act(sq.v(), pz.v(), AF.Square)
                        for b_ in range(4):
                            kb = kc * 4 + b_
                            mm(V(pb[2].t[:, kb:kb + 1], [pb[2].b()]), sq.v((slice(None), slice(b_ * 128, (b_ + 1) * 128))),
                               ones.v((slice(None), slice(0, 1))), signal=(b_ == 3))
                        yield
                    tt("dve", sstmp.v(), V(pb[2].t[:, 0:64], [pb[2].b()]), sskr.v(), ALU.add)
                    rsqrt_act(sstmp.v(), sstmp.v(), 1.0 / 192, epsT.v())
                    ts("dve", sck.v(), sstmp.v(), SC, ALU.mult)

                def v_gen():
                    for kg4 in range(16):
                        pz = pb[3 + kg4 % 2]
                        for b_ in range(4):
                            kb = kg4 * 4 + b_
                            for t in range(2):
                                mm(V(pz.t[:, b_ * 128:(b_ + 1) * 128], [pz.b()]), V(nT.t[:, t, kb * 128:(kb + 1) * 128], [nT.b()]),
                                   V(wkv_b.t[:, t, h * 256 + 128:h * 256 + 256], [wkv_b.b()]), start=(t == 0), stop=(t == 1),
                                   signal=(b_ == 3 and t == 1))
                        cp("dve" if kg4 % 2 else "act", V(Vh.t[:, kg4 * 4:(kg4 + 1) * 4, :], [Vh.b(kg4)]),
                           V(pz.t[:].rearrange("p (k c) -> p k c", c=128), [pz.b()]))
                        yield

                def q_gen():
                    for (qs, nq) in qchunks:
                        sl = (slice(None), slice(0, nq))
                        sl64 = (slice(0, 64), slice(0, nq))
                        S.dma("sp", cs2.v(sl64), V(cosT.t[:, OWN0 + qs:OWN0 + qs + nq], []))
                        S.dma("sp", sn2.v(sl64), V(sinT.t[:, OWN0 + qs:OWN0 + qs + nq], []))
                        pqn, pqr, pss = pb[5], pb[6], pb[7]
                        prot = pb[6]
                        for t in range(4):
                            mm(V(pqn.t[:, 0:nq], [pqn.b()]), V(wq_b.t[:, t, h * 192:h * 192 + 128], [wq_b.b()]),
                               V(qnT.t[:, t, qs:qs + nq], [qnT.b()]), start=(t == 0), stop=(t == 3), signal=(t == 3))
                        for t in range(4):
                            mm(V(pqr.t[0:64, 0:nq], [pqr.b()]), V(wq_b.t[:, t, h * 192 + 128:h * 192 + 192], [wq_b.b()]),
                               V(qnT.t[:, t, qs:qs + nq], [qnT.b()]), start=(t == 0), stop=(t == 3), signal=(t == 3))
                        yield
                        act(sqq.v(sl), V(pqn.t[:, 0:nq], [pqn.b()]), AF.Square)
                        act(sqr2.v(sl64), V(pqr.t[0:64, 0:nq], [pqr.b()]), AF.Square)
                        act(qg.v(sl64), V(pqr.t[0:64, 0:nq], [pqr.b()]), AF.Copy, scale=col(O_GQR, 1, 64))
                        yield
                        mm(V(pss.t[:, 0:nq], [pss.b()]), ones.v(), sqq.v(sl), start=True, stop=False, signal=False)
                        mm(V(pss.t[:, 0:nq], [pss.b()]), ones.v((slice(0, 64), slice(None))), sqr2.v(sl64), start=False, stop=True)
                        mm(V(prot.t[0:64, 0:nq], [prot.b()]), rotT.v(), qg.v(sl64))
                        yield
                        rsqrt_act(cq.v(sl), V(pss.t[:, 0:nq], [pss.b()]), 1.0 / 192, epsT.v())
                        tt("dve", u1.v(sl64), qg.v(sl64), cs2.v(sl64), ALU.mult)
                        tt("dve", u2.v(sl64), V(prot.t[0:64, 0:nq], [prot.b()]), sn2.v(sl64), ALU.mult)
                        yield
                        stt("dve", V(QN.t[:, qs:qs + nq], [QN.b(qs)]), V(pqn.t[:, 0:nq], [pqn.b()]), col(O_GQN), cq.v(sl),
                            ALU.mult, ALU.mult)
                        tt("dve", u1.v(sl64), u1.v(sl64), u2.v(sl64), ALU.add)
                        tt("dve", V(QR.t[0:64, qs:qs + nq], [QR.b(qs)]), u1.v(sl64), cq.v(sl64), ALU.mult)
                        yield

                gens = [k_gen(), v_gen(), q_gen()]
                while gens:
                    for gg in list(gens):
                        try:
                            next(gg)
                        except StopIteration:
                            gens.remove(gg)
                units = []
                for ci, (qs, nq) in enumerate(qchunks):
                    j0 = qs // 128
                    nqb = nq // 128
                    nfull = 31 + j0
                    order = [(0, None)] + [(31 + j0 + i, i) for i in range(nqb)] + [(kb, None) for kb in range(1, nfull)]
                    for oi, (kb, di) in enumerate(order):
                        units.append((ci, qs, nq, kb, di, oi == 0, oi == len(order) - 1))
                LA = 3
                for ui in range(len(units) + LA):
                    back = None
                    if ui >= LA:
                        uj = ui - LA
                        ci_b, qs_b, nq_b, kb_b, di_b, first_b, last_b = units[uj]
                        c0_b = 0 if di_b is None else di_b * 128
                        ptb = PT[uj % 4]
                        po_ = pb[4 + ci_b % 2]
                        pl_ = pb[6 + ci_b % 2]
                        cslb = (slice(None), slice(c0_b, nq_b))
                        S.prewait("pe", reads=[ptb.v(cslb), V(None, [Vh.b(kb_b // 4)])],
                                  writes=[V(None, [po_.b()]), V(None, [pl_.b()])])
                        back = True
                    if ui < len(units):
                        ci, qs, nq, kb, di, first, last = units[ui]
                        c0 = 0 if di is None else di * 128
                        ps_ = pb[ui % 4]
                        pt_ = PT[ui % 4]
                        csl = (slice(None), slice(c0, nq))
                        mm(V(ps_.t[:, c0:nq], [ps_.b()]), V(KT.t[:, kb * 128:(kb + 1) * 128], [KT.b(kb // 4)]),
                           V(QN.t[:, qs + c0:qs + nq], [QN.b(qs)]), start=True, stop=False, signal=False)
                        mm(V(ps_.t[:, c0:nq], [ps_.b()]), V(krT.t[:, kb * 128:(kb + 1) * 128], [krT.b()]),
                           V(QR.t[:, qs + c0:qs + nq], [QR.b(qs)]), start=False, stop=True)
                        act(pt_.v(csl), V(ps_.t[:, c0:nq], [ps_.b()]), AF.Exp,
                            scale=sck.v((slice(None), slice(kb, kb + 1))), bias=kbias.v((slice(None), slice(kb, kb + 1))))
                        if di is not None:
                            tt("dve", pt_.v((slice(None), slice(c0, c0 + 128))), pt_.v((slice(None), slice(c0, c0 + 128))),
                               tri.v(), ALU.mult)
                    gunit[0] += 1
                    if gunit[0] % 15 == 0:
                        issue_casts(1)
                    if back:
                        mm(V(po_.t[:, c0_b:nq_b], [po_.b()]), V(Vh.t[:, kb_b, :], [Vh.b(kb_b // 4)]), ptb.v(cslb),
                           start=first_b, stop=last_b, signal=False)
                        mm(V(pl_.t[:, c0_b:nq_b], [pl_.b()]), ones.v(), ptb.v(cslb), start=first_b, stop=last_b)
                        if last_b:
                            sl = (slice(None), slice(0, nq_b))
                            act(rl2.v(sl), V(pl_.t[:, 0:nq_b], [pl_.b()]), AF.Ln, bias=tinyT.v())
                            act(rl2.v(sl), rl2.v(sl), AF.Exp, scale=-1.0)
                            o_ = oh[ci_b % 2]
                            tt("dve", o_.v(sl), V(po_.t[:, 0:nq_b], [po_.b()]), rl2.v(sl), ALU.mult)
                            S.dma("sp", V(cat_d.t[:, 4 + h, qs_b:qs_b + nq_b], [cat_d.b()]), o_.v(sl))
                            ck(f"p2_h{h}_c{ci_b}")
                ck(f"p2_h{h}")
            issue_casts(1000)
            S.barrier()
        p12.close()

        with ExitStack() as s3:
            wo_sb = sb(s3, [128, 16, D], BF16, "wo_sb")
            uT = sb(s3, [128, NFT, 512], BF16, "uT")
            h2T = sb(s3, [128, 16, 512], BF16, "h2T")
            h2halo = sb(s3, [128, 16, 2], BF16, "h2halo")
            catc = [sb(s3, [128, 16, 128], BF16, f"catc{i}") for i in range(2)]
            wgs = [sb(s3, [128, 16, 128], BF16, f"wgs{i}") for i in range(2)]
            wus = [sb(s3, [128, 16, 128], BF16, f"wus{i}") for i in range(2)]
            wds = [sb(s3, [128, 4, 512], BF16, f"wds{i}") for i in range(3)]
            xt3 = sb(s3, [128, D], F32, "xt3")
            xb3 = sb(s3, [128, D], BF16, "xb3")
            ss3 = sb(s3, [128, 1], F32, "ss3")
            rs3 = sb(s3, [128, 1], F32, "rs3")
            gbuf = sb(s3, [128, 514], F32, "gbuf")
            acc = sb(s3, [128, 512], F32, "acc")
            sg = sb(s3, [128, 512], F32, "sg")
            ghist = sb(s3, [128, NFT, 2], F32, "ghist")
            outs = [sb(s3, [128, 512], F32, f"outs{i}") for i in range(2)]
            x1r = [sb(s3, [128, 512], F32, f"x1r{i}") for i in range(2)]
            with ExitStack() as s0:
                stage_w3 = [sb(s0, [128, 1024], F32, f"wst3_{i}") for i in range(2)]
                src_o = w_o.t.rearrange("(k p) c -> k p c", p=128)
                for k2 in range(32):
                    k, hh = k2 // 2, k2 % 2
                    S.dma("sp", stage_w3[k2 % 2].v(), V(src_o[k][:, hh * 1024:(hh + 1) * 1024], []))
                    cp("pool" if k2 % 2 else "dve", V(wo_sb.t[:, k, hh * 1024:(hh + 1) * 1024], [wo_sb.b((k, hh))]),
                       stage_w3[k2 % 2].v())
                S.barrier()

            x1r += [sb(s3, [128, 512], F32, f"x1r{i}") for i in (2, 3)]
            wds.append(sb(s3, [128, 4, 512], BF16, "wds3"))
            gcolb = V(cols.t[:, O_GFFN:O_GFFN + 16].unsqueeze(2).to_broadcast([128, 16, 128]), [cols.b()])
            gcolb8a = V(cols.t[:, O_GFFN:O_GFFN + 8].unsqueeze(2).to_broadcast([128, 8, 128]), [cols.b()])
            gcolb8b = V(cols.t[:, O_GFFN + 8:O_GFFN + 16].unsqueeze(2).to_broadcast([128, 8, 128]), [cols.b()])
            p3chunks = [(0, 128)] + [(128 + 512 * i, 512) for i in range(8)]
            wcnt = [0]
            blk3 = [0]
            for ci, (cs, n) in enumerate(p3chunks):
                nb_ = n // 128
                for b_ in range(nb_):
                    r0 = cs + b_ * 128
                    cc = catc[blk3[0] % 2]
                    blk3[0] += 1
                    S.dma("sp", cc.v(), V(cat_d.t[:, :, r0:r0 + 128], [cat_d.b()]))
                    S.dma("sp", xt3.v(), V(xk.t[OWN0 + r0:OWN0 + r0 + 128, :], []))
                    for dc in range(4):
                        pw = pb[2 + dc]
                        for f in range(16):
                            mm(pw.v(), V(cc.t[:, f, :], [cc.b()]), V(wo_sb.t[:, f, dc * 512:(dc + 1) * 512], [wo_sb.b((f, dc // 2))]),
                               start=(f == 0), stop=(f == 15), signal=(f == 15))
                        tt("dve", xt3.v((slice(None), slice(dc * 512, (dc + 1) * 512))), pw.v(),
                           xt3.v((slice(None), slice(dc * 512, (dc + 1) * 512))), ALU.add)
                    S.dma("sp", V(x1_d.t[r0:r0 + 128, :], [x1_d.b()]), xt3.v())
                    act(xb3.v(), xt3.v(), AF.Square, accum=ss3.v())
                    rsqrt_act(rs3.v(), ss3.v(), 1.0 / D, epsT.v())
                    ts("dve", xb3.v(), xt3.v(), rs3.v(), ALU.mult)
                    for k in range(16):
                        bank = pb[k // 8]
                        tr(V(pbf(k // 8)[:, k % 8, :], [bank.b()]), xb3.v((slice(None), slice(k * 128, (k + 1) * 128))),
                           idb.v(), signal=(k % 8 == 7))
                    c0 = b_ * 128
                    tt("dve", V(h2T.t[:, 0:8, c0:c0 + 128], [h2T.b(("a", c0))]), V(pbf(0), [pb[0].b()]), gcolb8a, ALU.mult)
                    tt("dve", V(h2T.t[:, 8:16, c0:c0 + 128], [h2T.b(("b", c0))]), V(pbf(1), [pb[1].b()]), gcolb8b, ALU.mult)
                if ci == 0:
                    cp("dve", h2halo.v(), V(h2T.t[:, :, 126:128], [h2T.b(("a", 0)), h2T.b(("b", 0))]))
                    ck("p3_c0")
                    continue

                def h2_rhs(k):
                    return V(h2T.t[:, k, :], [h2T.b((("a" if k < 8 else "b"), c0)) for c0 in range(0, 512, 128)])

                for j in range(NFT):
                    wg_ = wgs[j % 2]
                    wu_ = wus[j % 2]
                    S.dma("sp", wg_.v(), V(wg_b.t[j], [wg_b.b(j)]))
                    S.dma("sp", wu_.v(), V(wu_b.t[j], [wu_b.b(j)]))
                    if ci == 1:
                        ph = pb[6]
                        for k in range(16):
                            mm(V(ph.t[:, 0:2], [ph.b()]), V(wg_.t[:, k, :], [wg_.b()]), V(h2halo.t[:, k, :], [h2halo.b()]),
                               start=(k == 0), stop=(k == 15), signal=(k == 15))
                        act(V(ghist.t[:, j, :], [ghist.b(j)]), V(ph.t[:, 0:2], [ph.b()]), AF.Copy, scale=halo.v())
                    pg = pb[2 + j % 2]
                    pu = pb[4 + j % 2]
                    for k in range(16):
                        mm(pg.v(), V(wg_.t[:, k, :], [wg_.b()]), h2_rhs(k), start=(k == 0), stop=(k == 15), signal=(k == 15))
                    for k in range(16):
                        mm(pu.v(), V(wu_.t[:, k, :], [wu_.b()]), h2_rhs(k), start=(k == 0), stop=(k == 15), signal=(k == 15))
                    cp("pool", gbuf.v((slice(None), slice(0, 2))), V(ghist.t[:, j, :], [ghist.b(j)]))
                    cp("act", gbuf.v((slice(None), slice(2, 514))), pg.v())
                    act(acc.v(), pg.v(), AF.Identity, scale=col(O_CW + 2 * NFT + j), bias=col(O_CB + j))
                    cp("pool", V(ghist.t[:, j, :], [ghist.b(j)]), gbuf.v((slice(None), slice(512, 514))))
                    stt("dve", acc.v(), gbuf.v((slice(None), slice(1, 513))), col(O_CW + NFT + j), acc.v(), ALU.mult, ALU.add)
                    stt("dve", acc.v(), gbuf.v((slice(None), slice(0, 512))), col(O_CW + j), acc.v(), ALU.mult, ALU.add)
                    act(sg.v(), acc.v(), AF.Silu)
                    tt("dve", V(uT.t[:, j, :], [uT.b(j)]), sg.v(), pu.v(), ALU.mult)
                for dc in range(4):
                    for b_ in range(4):
                        r0 = cs + b_ * 128
                        S.dma("sp", x1r[b_].v(), V(x1_d.t[r0:r0 + 128, dc * 512:(dc + 1) * 512], [x1_d.b()]))
                    for jg in range(NFT // 4):
                        wd_ = wds[wcnt[0] % 4]
                        wcnt[0] += 1
                        S.dma("sp", wd_.v(), V(wd_b.t[jg * 4:(jg + 1) * 4, :, dc * 512:(dc + 1) * 512].rearrange("j p c -> p j c"),
                                               [wd_b.b(jg * 4 + i) for i in range(4)]))
                        for jj in range(4):
                            j = jg * 4 + jj
                            for b_ in range(4):
                                pd = pb[(0, 1, 6, 7)[b_]]
                                mm(pd.v(), V(uT.t[:, j, b_ * 128:(b_ + 1) * 128], [uT.b(j)]), V(wd_.t[:, jj, :], [wd_.b()]),
                                   start=(j == 0), stop=(j == NFT - 1), signal=(j == NFT - 1 or (jj == 3 and b_ == 3)))
                    for b_ in range(4):
                        pd = pb[(0, 1, 6, 7)[b_]]
                        r0 = cs + b_ * 128
                        xr = x1r[b_]
                        o_ = outs[b_ % 2]
                        tt("dve", o_.v(), pd.v(), xr.v(), ALU.add)
                        S.dma("act", V(out_t.t[r0 - 128:r0, dc * 512:(dc + 1) * 512], [out_t.b()]), o_.v())
                ck(f"p3_c{ci}")
            S.finish()


_NC_CACHE = {}


def _host_tables(core):
    half = 32
    inv_freq = (1.0 / (10000.0 ** (np.arange(half, dtype=np.float32) / np.float32(half)))).astype(np.float32)
    base = -4096 if core == 0 else 0
    pos = (np.arange(TK) + base).astype(np.float32)
    ang = pos[None, :] * inv_freq[:, None]
    cos = np.cos(ang).astype(np.float32)
    sin = np.sin(ang).astype(np.float32)
    cosT = np.concatenate([cos, cos], 0)
    sinT = np.concatenate([sin, sin], 0)
    kbias = np.zeros((128, 64), np.float32)
    if core == 0:
        kbias[:, :32] = -30000.0
    halo = np.full((128, 1), 0.0 if core == 0 else 1.0, np.float32)
    pinv = np.zeros((128, 4, P1N), np.float32)
    t = np.arange(P1N)
    for g, w in enumerate((2, 4, 8, 16)):
        cnt = np.minimum(t + 1, w) if core == 0 else np.full(P1N, w)
        pinv[:, g, :] = (1.0 / cnt.astype(np.float32))[None, :]
    return cosT, sinT, kbias, halo, pinv


def kernel(**inputs):
    x = np.asarray(inputs["x"], np.float32)
    mem = np.asarray(inputs["mem"], np.float32)
    if "nc" not in _NC_CACHE:
        _NC_CACHE["nc"] = build_nc()
    nc = _NC_CACHE["nc"]
    rot = np.zeros((64, 64), np.float32)
    for m in range(32):
        rot[m + 32, m] = -1.0
        rot[m, m + 32] = 1.0
    tri = np.triu(np.ones((128, 128), np.float32))
    shared = {
        "ident_f": np.eye(128, dtype=np.float32), "rotT": rot, "tri": tri,
    }
    for name in ("g_mix", "g_ffn", "g_mem", "g_q_lat", "g_kv_lat", "g_q_mla", "g_k_mla", "pool_scale", "g_q_x", "g_k_x",
                 "conv_w", "conv_b", "w_in", "w_q_up", "w_kv_up", "w_pool", "w_mem_kv", "w_o", "w_gate", "w_up", "w_down"):
        shared[name] = np.ascontiguousarray(np.asarray(inputs[name], np.float32)[0])
    tabs = [_host_tables(0), _host_tables(1)]
    in_maps = []
    for b in range(NB):
        for c in range(2):
            if c == 0:
                xk = np.concatenate([np.zeros((4096, D), np.float32), x[b, :4096]], 0)
            else:
                xk = x[b]
            cosT, sinT, kbias, halo, pinv = tabs[c]
            m = dict(shared)
            m.update({"xk": np.ascontiguousarray(xk), "mem": np.ascontiguousarray(mem[b]), "cosT": cosT, "sinT": sinT,
                      "kbias": kbias, "halo_flag": halo, "pool_inv": pinv})
            in_maps.append(m)
    res = run_bass_kernel_spmd(nc, in_maps, core_ids=list(range(8)))
    out = np.empty((NB, SEQ, D), np.float32)
    for b in range(NB):
        for c in range(2):
            out[b, c * 4096:(c + 1) * 4096] = np.asarray(res.results[b * 2 + c]["out"], np.float32)
    return out
```

```python
import math
from contextlib import ExitStack

import numpy as np
import concourse.bass as bass
import concourse.mybir as mybir
from concourse.bass_utils import run_bass_kernel_spmd

F32 = mybir.dt.float32
BF16 = mybir.dt.bfloat16
ALU = mybir.AluOpType
AF = mybir.ActivationFunctionType

D = 2048
SEQ = 8192
NB = 4
MEM = 256
IN_COLS = 1856
DFF = 5632
NFT = DFF // 128
TK = 8192
OWN0 = 3968
NOWN = 4224
EPS = 1e-6
P1N = 256
C_POOL, C_Q, C_KV, C_KR, C_MQ = 0, 512, 1024, 1280, 1344


class Buf:
    __slots__ = ("name", "w", "r", "multi")

    def __init__(self, name, multi=False):
        self.name = name
        self.w = {}
        self.r = {}
        self.multi = multi


class Ev:
    __slots__ = ("key", "val", "clock")

    def __init__(self, key, val, clock):
        self.key = key
        self.val = val
        self.clock = clock


class V:
    __slots__ = ("ap", "bufs")

    def __init__(self, ap, bufs):
        self.ap = ap
        self.bufs = bufs


class Tile:
    def __init__(self, t, name, multi=False):
        self.t = t
        self.name = name
        self.multi = multi
        self.bufs = {}

    def b(self, k=None):
        if k not in self.bufs:
            self.bufs[k] = Buf(f"{self.name}.{k}", self.multi)
        return self.bufs[k]

    def v(self, idx=None, b=None):
        ap = self.t[:] if idx is None else self.t[idx]
        if b == "*":
            bufs = list(self.bufs.values())
        elif isinstance(b, (list, tuple)):
            bufs = [self.b(x) for x in b]
        else:
            bufs = [self.b(b)]
        return V(ap, bufs)


class Sched:
    EPOCH = 8000
    NDMA = 40

    def __init__(self, nc, stack):
        self.nc = nc
        self.stack = stack
        self.eng = {"pe": nc.tensor, "act": nc.scalar, "dve": nc.vector,
                    "pool": nc.gpsimd, "sp": nc.sync}
        self.count = {e: 0 for e in self.eng}
        self.clock = {e: {} for e in self.eng}
        self.sems = {e: [] for e in self.eng}
        self.dma_sems = [stack.enter_context(nc.semaphore(f"dq{i}")) for i in range(self.NDMA)]
        self.dma_val = [0] * self.NDMA
        self.dma_last = [None] * self.NDMA
        self.dma_i = 0
        self.nwaits = 0
        self.ninst = 0
        self.pending = {e: False for e in self.eng}

    def _sem(self, e, cnt):
        ep = (cnt - 1) // self.EPOCH
        while len(self.sems[e]) <= ep:
            self.sems[e].append(self.stack.enter_context(
                self.nc.semaphore(f"s_{e}_{len(self.sems[e])}")))
        return self.sems[e][ep], (cnt - 1) % self.EPOCH + 1

    def _wait_ev(self, e, ev):
        eng = self.eng[e]
        if isinstance(ev.key, tuple):
            eng.wait_ge(self.dma_sems[ev.key[1]], ev.val)
        else:
            sem, v = self._sem(ev.key, ev.val)
            eng.wait_ge(sem, v)
        self.nwaits += 1

    def _wait(self, e, deps):
        clk = self.clock[e]
        best = {}
        for ev in deps:
            if ev is None or clk.get(ev.key, 0) >= ev.val:
                continue
            o = best.get(ev.key)
            if o is None or o.val < ev.val:
                best[ev.key] = ev
        for ev in sorted(best.values(), key=lambda x: -len(x.clock)):
            if clk.get(ev.key, 0) >= ev.val:
                continue
            self._wait_ev(e, ev)
            for k, v in ev.clock.items():
                if clk.get(k, 0) < v:
                    clk[k] = v
            clk[ev.key] = ev.val

    def _deps(self, e, reads, writes):
        deps = []
        same_raw = e in ("act", "dve", "pool")
        for b in reads:
            for ev in b.w.values():
                if ev.key == e and not same_raw:
                    continue
                deps.append(ev)
        for b in writes:
            if b.multi:
                continue
            for ev in b.w.values():
                if ev.key != e or same_raw:
                    deps.append(ev)
            for ev in b.r.values():
                if ev.key != e or same_raw:
                    deps.append(ev)
        return deps

    def _record(self, ev, reads, writes):
        for b in reads:
            o = b.r.get(ev.key)
            if o is None or o.val < ev.val:
                b.r[ev.key] = ev
        for b in writes:
            if b.multi:
                o = b.w.get(ev.key)
                if o is None or o.val < ev.val:
                    b.w[ev.key] = ev
            else:
                b.w = {ev.key: ev}
                b.r = {}

    def op(self, e, fn, reads=(), writes=(), signal=True):
        rb = [b for v in reads for b in v.bufs]
        wb = [b for v in writes for b in v.bufs]
        self._wait(e, self._deps(e, rb, wb))
        inst = fn(self.eng[e])
        self.ninst += 1
        if signal:
            self.count[e] += 1
            sem, _ = self._sem(e, self.count[e])
            inst.then_inc(sem, 1)
            ev = Ev(e, self.count[e], dict(self.clock[e]))
            self.pending[e] = False
        else:
            ev = Ev(e, self.count[e] + 1, {})
            self.pending[e] = True
        self._record(ev, rb, wb)
        return ev

    def dma(self, q, out, in_, **kw):
        rb = list(in_.bufs)
        wb = list(out.bufs)
        deps = self._deps(q, rb, wb)
        i = self.dma_i
        self.dma_i = (i + 1) % self.NDMA
        deps.append(self.dma_last[i])
        self._wait(q, deps)
        self.dma_val[i] += 16
        self.eng[q].dma_start(out=out.ap, in_=in_.ap, **kw).then_inc(self.dma_sems[i], 16)
        self.ninst += 1
        ev = Ev(("d", i), self.dma_val[i], dict(self.clock[q]))
        self.dma_last[i] = ev
        self._record(ev, rb, wb)
        return ev

    def prewait(self, e, reads=(), writes=()):
        rb = [b for v in reads for b in v.bufs]
        wb = [b for v in writes for b in v.bufs]
        self._wait(e, self._deps(e, rb, wb))

    def barrier(self):
        assert not any(self.pending.values()), self.pending
        evs = [Ev(e, self.count[e], {}) for e in ("pe", "act", "dve", "pool") if self.count[e] > 0]
        evs += [x for x in self.dma_last if x is not None]
        for e in self.eng:
            self._wait(e, [x for x in evs if x.key != e])

    def finish(self):
        assert not any(self.pending.values()), self.pending
        self._wait("sp", [x for x in self.dma_last if x is not None])


class _Stop(Exception):
    pass


def build_nc(debug=None, stop=None):
    nc = bass.Bass("TRN2", target_bir_lowering=False)

    def ck(name):
        if stop == name:
            raise _Stop()

    def din(name, shape, dt=F32):
        return Tile(nc.dram_tensor(name, shape, dt, kind="ExternalInput").ap(), name, multi=True)

    def dscr(name, shape, dt):
        return Tile(nc.dram_tensor(name, shape, dt).ap(), name, multi=True)

    xk = din("xk", [TK, D])
    mem = din("mem", [MEM, D])
    g_mix = din("g_mix", [D]); g_ffn = din("g_ffn", [D]); g_mem = din("g_mem", [D])
    g_q_lat = din("g_q_lat", [512]); g_kv_lat = din("g_kv_lat", [256])
    g_q_mla = din("g_q_mla", [192]); g_k_mla = din("g_k_mla", [192])
    pool_scale = din("pool_scale", [512]); g_q_x = din("g_q_x", [128]); g_k_x = din("g_k_x", [128])
    conv_w = din("conv_w", [3, DFF]); conv_b = din("conv_b", [DFF])
    w_in = din("w_in", [D, IN_COLS]); w_q_up = din("w_q_up", [512, 1536]); w_kv_up = din("w_kv_up", [256, 2048])
    w_pool = din("w_pool", [4, 128, 128]); w_mem_kv = din("w_mem_kv", [D, 1024]); w_o = din("w_o", [D, D])
    w_gate = din("w_gate", [D, DFF]); w_up = din("w_up", [D, DFF]); w_down = din("w_down", [DFF, D])
    cosT = din("cosT", [64, TK]); sinT = din("sinT", [64, TK])
    kbias_d = din("kbias", [128, 64]); halo_d = din("halo_flag", [128, 1]); pinv_d = din("pool_inv", [128, 4, P1N])
    ident_d = din("ident_f", [128, 128]); rot_d = din("rotT", [64, 64]); tri_d = din("tri", [128, 128])
    out_t = Tile(nc.dram_tensor("out", [4096, D], F32, kind="ExternalOutput").ap(), "out", multi=True)

    qn_d = dscr("qn_d", [128, 4, NOWN], BF16)
    cat_d = dscr("cat_d", [128, 16, NOWN], BF16)
    x1_d = dscr("x1_d", [NOWN, D], F32)
    wg_b = dscr("wg_b", [NFT, 128, 16, 128], BF16)
    wu_b = dscr("wu_b", [NFT, 128, 16, 128], BF16)
    wd_b = dscr("wd_b", [NFT, 128, D], BF16)
    dbg = None
    if debug:
        dbg = Tile(nc.dram_tensor("dbg", list(debug), F32, kind="ExternalOutput").ap(), "dbg", multi=True)

    with ExitStack() as st:
        S = Sched(nc, st)
        try:
            _program(nc, st, S, ck, locals())
        except _Stop:
            S.finish()
            st.pop_all()
        print("instructions", S.ninst, "waits", S.nwaits, {e: S.count[e] for e in S.count})
    return nc


def _program(nc, st, S, ck, G):
    globals_ = G
    xk = G["xk"]; mem = G["mem"]; g_mix = G["g_mix"]; g_ffn = G["g_ffn"]; g_mem = G["g_mem"]
    g_q_lat = G["g_q_lat"]; g_kv_lat = G["g_kv_lat"]; g_q_mla = G["g_q_mla"]; g_k_mla = G["g_k_mla"]
    pool_scale = G["pool_scale"]; g_q_x = G["g_q_x"]; g_k_x = G["g_k_x"]; conv_w = G["conv_w"]; conv_b = G["conv_b"]
    w_in = G["w_in"]; w_q_up = G["w_q_up"]; w_kv_up = G["w_kv_up"]; w_pool = G["w_pool"]; w_mem_kv = G["w_mem_kv"]
    w_o = G["w_o"]; w_gate = G["w_gate"]; w_up = G["w_up"]; w_down = G["w_down"]; cosT = G["cosT"]; sinT = G["sinT"]
    kbias_d = G["kbias_d"]; halo_d = G["halo_d"]; pinv_d = G["pinv_d"]; ident_d = G["ident_d"]; rot_d = G["rot_d"]
    tri_d = G["tri_d"]; out_t = G["out_t"]; qn_d = G["qn_d"]; cat_d = G["cat_d"]; x1_d = G["x1_d"]
    wg_b = G["wg_b"]; wu_b = G["wu_b"]; wd_b = G["wd_b"]; dbg = G["dbg"]; debug = G["debug"]
    if True:

        def sb(stack, shape, dt, name, multi=False):
            return Tile(stack.enter_context(nc.sbuf_tensor("sb_" + name, shape, dt)), name, multi)

        def mm(o, l, r, start=True, stop=True, signal=True):
            return S.op("pe", lambda e: e.matmul(o.ap, lhsT=l.ap, rhs=r.ap, start=start, stop=stop),
                        reads=[l, r], writes=[o], signal=signal)

        def tr(o, i, ident, signal=True):
            return S.op("pe", lambda e: e.transpose(out=o.ap, in_=i.ap, identity=ident.ap),
                        reads=[i, ident], writes=[o], signal=signal)

        def act(o, i, func, scale=1.0, bias=None, accum=None):
            rd = [i]
            kw = {}
            if isinstance(scale, V):
                rd.append(scale); kw["scale"] = scale.ap
            else:
                kw["scale"] = scale
            if bias is not None:
                if isinstance(bias, V):
                    rd.append(bias); kw["bias"] = bias.ap
                else:
                    kw["bias"] = bias
            wr = [o]
            if accum is not None:
                wr.append(accum); kw["accum_out"] = accum.ap
            return S.op("act", lambda e: e.activation(out=o.ap, in_=i.ap, func=func, **kw), reads=rd, writes=wr)

        def tt(eng, o, a, b_, op):
            return S.op(eng, lambda e: e.tensor_tensor(out=o.ap, in0=a.ap, in1=b_.ap, op=op), reads=[a, b_], writes=[o])

        def ts(eng, o, a, s1, op0, s2=None, op1=None):
            rd = [a]
            if isinstance(s1, V):
                rd.append(s1); s1a = s1.ap
            else:
                s1a = s1
            if isinstance(s2, V):
                rd.append(s2); s2a = s2.ap
            else:
                s2a = s2
            kw = {} if op1 is None else {"op1": op1}
            return S.op(eng, lambda e: e.tensor_scalar(out=o.ap, in0=a.ap, scalar1=s1a, scalar2=s2a, op0=op0, **kw),
                        reads=rd, writes=[o])

        def stt(eng, o, a, s, b_, op0, op1):
            rd = [a, b_]
            if isinstance(s, V):
                rd.append(s); sa = s.ap
            else:
                sa = s
            return S.op(eng, lambda e: e.scalar_tensor_tensor(out=o.ap, in0=a.ap, scalar=sa, in1=b_.ap, op0=op0, op1=op1),
                        reads=rd, writes=[o])

        def cp(eng, o, i):
            if eng == "act":
                return S.op("act", lambda e: e.copy(out=o.ap, in_=i.ap), reads=[i], writes=[o])
            return S.op(eng, lambda e: e.tensor_copy(out=o.ap, in_=i.ap), reads=[i], writes=[o])

        def memset(eng, o, val):
            return S.op(eng, lambda e: e.memset(o.ap, val), writes=[o])

        def recip(o, i):
            return S.op("dve", lambda e: e.reciprocal(out=o.ap, in_=i.ap), reads=[i], writes=[o])

        def rsqrt_act(o, i, inv_n, eps_v):
            act(o, i, AF.Ln, scale=inv_n, bias=eps_v)
            act(o, o, AF.Exp, scale=-0.5)

        pb = [Tile(st.enter_context(nc.psum_tensor(f"pb{i}", [128, 512], F32)), f"pb{i}") for i in range(8)]

        def pbf(i):
            return pb[i].t[:].bitcast(BF16).rearrange("p (k c) -> p k c", c=128)

        idf = sb(st, [128, 128], F32, "idf")
        idb = sb(st, [128, 128], BF16, "idb")
        ones = sb(st, [128, 128], BF16, "ones")
        rotT = sb(st, [64, 64], F32, "rotT")
        tri = sb(st, [128, 128], BF16, "tri")
        epsT = sb(st, [128, 1], F32, "epsT")
        kbias = sb(st, [128, 64], F32, "kbias")
        halo = sb(st, [128, 1], F32, "halo")
        NCOL = 16 * 3 + 4 + 2 + 4 + 4 + 4 + 3 * NFT + NFT
        cols = sb(st, [128, NCOL], F32, "cols")
        O_GMIX, O_GFFN, O_GMEM = 0, 16, 32
        O_GQL, O_GKVL = 48, 52
        O_GQN, O_GQR, O_GKN, O_GKR = 54, 55, 56, 57
        O_PS = 58
        O_GQX, O_GKX = 62, 63
        O_CW = 66
        O_CB = 66 + 3 * NFT

        def col(o, n=1, p=128):
            return cols.v((slice(0, p), slice(o, o + n)))

        S.dma("sp", idf.v(), ident_d.v())
        cp("dve", idb.v(), idf.v())
        memset("dve", ones.v(), 1.0)
        memset("dve", epsT.v(), EPS)
        S.dma("sp", rotT.v(), rot_d.v())
        S.dma("sp", kbias.v(), kbias_d.v())
        S.dma("sp", halo.v(), halo_d.v())

        with ExitStack() as s0:
            stg = sb(s0, [128, 128], F32, "stg")
            S.dma("sp", stg.v(), tri_d.v())
            cp("dve", tri.v(), stg.v())

            def vec_cols(vec_ap, r, w, off):
                S.dma("sp", stg.v((slice(0, r), slice(0, w))), V(vec_ap, []))
                tr(V(pb[0].t[0:w, 0:r], [pb[0].b()]), stg.v((slice(0, r), slice(0, w))), idf.v((slice(0, r), slice(0, r))))
                cp("dve", cols.v((slice(0, w), slice(off, off + r))), V(pb[0].t[0:w, 0:r], [pb[0].b()]))

            def r128(t):
                return t.t.rearrange("(r p) -> r p", p=128)

            vec_cols(r128(g_mix), 16, 128, O_GMIX)
            vec_cols(r128(g_ffn), 16, 128, O_GFFN)
            vec_cols(r128(g_mem), 16, 128, O_GMEM)
            vec_cols(r128(g_q_lat), 4, 128, O_GQL)
            vec_cols(r128(g_kv_lat), 2, 128, O_GKVL)
            vec_cols(g_q_mla.t[0:128].rearrange("(r p) -> r p", p=128), 1, 128, O_GQN)
            vec_cols(g_q_mla.t[128:192].rearrange("(r p) -> r p", p=64), 1, 64, O_GQR)
            vec_cols(g_k_mla.t[0:128].rearrange("(r p) -> r p", p=128), 1, 128, O_GKN)
            vec_cols(g_k_mla.t[128:192].rearrange("(r p) -> r p", p=64), 1, 64, O_GKR)
            vec_cols(r128(pool_scale), 4, 128, O_PS)
            vec_cols(r128(g_q_x), 1, 128, O_GQX)
            vec_cols(r128(g_k_x), 1, 128, O_GKX)
            for j in range(3):
                vec_cols(conv_w.t[j].rearrange("(r p) -> r p", p=128), NFT, 128, O_CW + j * NFT)
            vec_cols(r128(conv_b), NFT, 128, O_CB)
            S.barrier()
        ck("p0")

        p12 = st.enter_context(ExitStack())
        nT = sb(p12, [128, 2, TK], BF16, "nT", multi=True)
        krT = sb(p12, [128, TK], BF16, "krT", multi=True)
        S.op("pool", lambda e: e.memset(krT.t[64:128, :], 0.0), writes=[V(None, [krT.b()])])
        sskr = sb(p12, [128, 64], F32, "sskr", multi=True)

        cast_jobs = []
        for wsrc, wdst in ((w_gate, wg_b), (w_up, wu_b)):
            for k in range(16):
                for q in range(4):
                    cast_jobs.append(("gu", wsrc, wdst, k, q))
        for j in range(NFT):
            for hh in range(2):
                cast_jobs.append(("d", w_down, wd_b, j, hh))
        cast_pos = [0]

        with ExitStack() as s1:
            winb = sb(s1, [128, 16, IN_COLS], BF16, "winb")
            xts = [sb(s1, [128, D], F32, f"xt{i}") for i in range(2)]
            xb = sb(s1, [128, D], BF16, "xb")
            ssx = sb(s1, [128, 1], F32, "ssx")
            rsx = sb(s1, [128, 1], F32, "rsx")
            hT = sb(s1, [128, 16, P1N], BF16, "hT")
            zts = [sb(s1, [128, P1N], F32, f"zt{i}") for i in range(4)]
            sqs = [sb(s1, [128, P1N], BF16, f"sq{i}") for i in range(2)]
            rss = [sb(s1, [128, P1N], F32, f"rs{i}") for i in range(2)]
            pbufs = [sb(s1, [128, 16 + P1N], F32, f"pbuf{i}") for i in range(4)]
            phist = [sb(s1, [128, 16], F32, f"phist{i}") for i in range(4)]
            ptmp = [sb(s1, [128, 16 + P1N], F32, f"ptmp{i}") for i in range(2)]
            dbf = [sb(s1, [128, P1N], BF16, f"dbf{i}") for i in range(2)]
            ycat = [sb(s1, [128, P1N], BF16, f"ycat{i}") for i in range(2)]
            ycx = [sb(s1, [128, P1N], BF16, f"ycx{i}") for i in range(2)]
            qn = sb(s1, [128, 4, P1N], BF16, "qn")
            KxT = sb(s1, [128, 4, MEM], BF16, "KxT")
            Vx = sb(s1, [128, 2, 512], BF16, "Vx")
            qx = sb(s1, [128, P1N], BF16, "qx")
            pexp = [sb(s1, [128, P1N], BF16, f"pexp{i}") for i in range(2)]
            rl1 = sb(s1, [128, P1N], F32, "rl1")
            sqr1 = sb(s1, [64, P1N], BF16, "sqr1")
            kg = sb(s1, [64, P1N], F32, "kg")
            cs_t = sb(s1, [64, P1N], F32, "cs_t")
            sn_t = sb(s1, [64, P1N], F32, "sn_t")
            t1 = sb(s1, [64, P1N], F32, "t1")
            t2 = sb(s1, [64, P1N], F32, "t2")
            pinv = sb(s1, [128, 4, P1N], F32, "pinv")
            wpool = sb(s1, [128, 4, 128], BF16, "wpool")
            S.dma("sp", pinv.v(), pinv_d.v())
            for g in range(4):
                memset("pool", phist[g].v(), 0.0)

            zi = [0]

            def next_zt():
                zi[0] += 1
                return zts[zi[0] % 4]

            pz_i = [0]

            def next_pz():
                pz_i[0] += 1
                return pb[2 + pz_i[0] % 4]

            xbs = [xb]
            hTs = [hT]
            curb = {"hT": hT, "xb": xb}

            def norm_pre(src_tile, row0, xt, xb):
                S.dma("sp", xt.v(), V(src_tile.t[row0:row0 + 128, :], []))
                act(xb.v(), xt.v(), AF.Square, accum=ssx.v())
                rsqrt_act(rsx.v(), ssx.v(), 1.0 / D, epsT.v())
                ts("dve", xb.v(), xt.v(), rsx.v(), ALU.mult)

            def norm_post(xb):
                for k in range(16):
                    bank = pb[k // 8]
                    tr(V(pbf(k // 8)[:, k % 8, :], [bank.b()]), xb.v((slice(None), slice(k * 128, (k + 1) * 128))),
                       idb.v(), signal=(k % 8 == 7))

            def norm_block(src_tile, row0, xt, gain_none=True):
                norm_pre(src_tile, row0, xt, curb["xb"])
                norm_post(curb["xb"])

            def evac_hT(dst, c0):
                cp("dve", V(dst.t[:, 0:8, c0:c0 + 128], [dst.b(("a", c0))]), V(pbf(0), [pb[0].b()]))
                cp("act", V(dst.t[:, 8:16, c0:c0 + 128], [dst.b(("b", c0))]), V(pbf(1), [pb[1].b()]))

            def hT_rhs(k, n):
                hT = curb["hT"]
                return V(hT.t[:, k, 0:n], [hT.b((("a" if k < 8 else "b"), c0)) for c0 in range(0, n, 128)])

            def proj(c0, wd, n, evac_eng):
                pz = next_pz()
                for k in range(16):
                    mm(V(pz.t[0:wd, 0:n], [pz.b()]), V(winb.t[:, k, c0:c0 + wd], [winb.b(k)]), hT_rhs(k, n),
                       start=(k == 0), stop=(k == 15), signal=(k == 15))
                z = next_zt()
                cp(evac_eng, z.v((slice(0, wd), slice(0, n))), V(pz.t[0:wd, 0:n], [pz.b()]))
                return z

            def stage_weight(s_, dst_fn, src_rows, ncols, k_tiles, gain_off, tag):
                stgs = [sb(s_, [128, ncols], F32, f"wst_{tag}{i}") for i in range(2)]
                for k in range(k_tiles):
                    stg_ = stgs[k % 2]
                    S.dma("sp", stg_.v(), V(src_rows[k], []))
                    if gain_off is None:
                        cp("dve", dst_fn(k), stg_.v())
                    else:
                        ts("dve", dst_fn(k), stg_.v(), col(gain_off + k), ALU.mult)

            with ExitStack() as s0:
                stage_weight(s0, lambda k: V(winb.t[:, k, 0:1024], [winb.b(k)]),
                             w_mem_kv.t.rearrange("(k p) c -> k p c", p=128), 1024, 16, O_GMEM, "m")
                stgp = sb(s0, [128, 4, 128], F32, "wst_p")
                S.dma("sp", stgp.v(), V(w_pool.t.rearrange("g c d -> c g d"), []))
                cp("dve", wpool.v(), stgp.v())
                for mb in range(2):
                    norm_block(mem, mb * 128, xts[mb])
                    evac_hT(hT, mb * 128)
                for h in range(4):
                    z = proj(h * 128, 128, MEM, "act")
                    sq = sqs[h % 2]
                    act(sq.v(), z.v(), AF.Square)
                    mm(V(pb[6].t[:, 0:MEM], [pb[6].b()]), ones.v(), sq.v())
                    rs = rss[h % 2]
                    rsqrt_act(rs.v(), V(pb[6].t[:, 0:MEM], [pb[6].b()]), 1.0 / 128, epsT.v())
                    stt("dve", V(KxT.t[:, h, :], [KxT.b(h)]), z.v(), col(O_GKX), rs.v(), ALU.mult, ALU.mult)
                for mb in range(2):
                    pz = next_pz()
                    for k in range(16):
                        mm(V(pz.t[:, 0:512], [pz.b()]),
                           V(hT.t[:, k, mb * 128:(mb + 1) * 128], [hT.b((("a" if k < 8 else "b"), mb * 128))]),
                           V(winb.t[:, k, 512:1024], [winb.b(k)]), start=(k == 0), stop=(k == 15), signal=(k == 15))
                    cp("act", V(Vx.t[:, mb, :], [Vx.b(mb)]), V(pz.t[:, 0:512], [pz.b()]))
                stage_weight(s0, lambda k: V(winb.t[:, k, :], [winb.b(k)]),
                             w_in.t.rearrange("(k p) c -> k p c", p=128), IN_COLS, 16, O_GMIX, "i")
                S.barrier()
            ck("mem")
            hTs.append(sb(s1, [128, 16, P1N], BF16, "hT2"))
            xbs.append(sb(s1, [128, D], BF16, "xb2"))

            chunks = [(0, 128)] + [(128 + P1N * i, P1N) for i in range(15)] + [(OWN0, 128)] + \
                     [(OWN0 + 128 + P1N * i, P1N) for i in range(16)]
            assert chunks[15][0] + chunks[15][1] == OWN0 and chunks[-1][0] + chunks[-1][1] == TK
            blk_i = 0
            for ci, (s, n) in enumerate(chunks):
                own = s >= OWN0
                os_ = s - OWN0
                first_real = (s == OWN0 + 128)
                nb_ = n // 128
                curb["hT"] = hTs[ci % 2]
                if ci == 0:
                    for b_ in range(nb_):
                        norm_pre(xk, s + b_ * 128, xts[b_], xbs[b_])
                    for b_ in range(nb_):
                        norm_post(xbs[b_])
                        evac_hT(hTs[0], b_ * 128)
                if ci + 1 < len(chunks):
                    s_n, n_n = chunks[ci + 1]
                    for b_ in range(n_n // 128):
                        norm_pre(xk, s_n + b_ * 128, xts[b_], xbs[b_])
                S.dma("sp", cs_t.v((slice(None), slice(0, n))), V(cosT.t[:, s:s + n], []))
                S.dma("sp", sn_t.v((slice(None), slice(0, n))), V(sinT.t[:, s:s + n], []))
                sl = (slice(None), slice(0, n))
                sl64 = (slice(0, 64), slice(0, n))

                def kv_gen():
                    zkv = []
                    for t in range(2):
                        zkv.append(proj(C_KV + t * 128, 128, n, "act" if t == 0 else "dve"))
                        yield
                    for t in range(2):
                        act(sqs[t].v(sl), zkv[t].v(sl), AF.Square)
                        mm(V(pb[6].t[:, 0:n], [pb[6].b()]), ones.v(), sqs[t].v(sl), start=(t == 0), stop=(t == 1),
                           signal=(t == 1))
                    yield
                    rsqrt_act(rss[0].v(sl), V(pb[6].t[:, 0:n], [pb[6].b()]), 1.0 / 256, epsT.v())
                    yield
                    for t in range(2):
                        stt("dve", V(nT.t[:, t, s:s + n], [nT.b()]), zkv[t].v(sl), col(O_GKVL + t), rss[0].v(sl),
                            ALU.mult, ALU.mult)

                def kr_gen():
                    zkr = proj(C_KR, 64, n, "act")
                    yield
                    act(sqr1.v(sl64), zkr.v(sl64), AF.Square)
                    for b_ in range(nb_):
                        mm(V(pb[7].t[:, b_:b_ + 1], [pb[7].b()]), sqr1.v((slice(0, 64), slice(b_ * 128, (b_ + 1) * 128))),
                           ones.v((slice(0, 64), slice(0, 1))), signal=(b_ == nb_ - 1))
                    cp("dve", V(sskr.t[:, s // 128:s // 128 + nb_], [sskr.b()]), V(pb[7].t[:, 0:nb_], [pb[7].b()]))
                    ts("dve", kg.v(sl64), zkr.v(sl64), col(O_GKR, 1, 64), ALU.mult)
                    yield
                    mm(V(pb[7].t[0:64, 0:n], [pb[7].b()]), rotT.v(), kg.v(sl64))
                    tt("dve", t1.v(sl64), kg.v(sl64), cs_t.v(sl64), ALU.mult)
                    yield
                    tt("dve", t2.v(sl64), V(pb[7].t[0:64, 0:n], [pb[7].b()]), sn_t.v(sl64), ALU.mult)
                    tt("dve", V(krT.t[0:64, s:s + n], [krT.b()]), t1.v(sl64), t2.v(sl64), ALU.add)

                gens = [kv_gen(), kr_gen()]
                while gens:
                    for gg in list(gens):
                        try:
                            next(gg)
                        except StopIteration:
                            gens.remove(gg)
                if own:
                    def pool_gen(g):
                        w = 2 << g
                        pz = next_pz()
                        for k in range(16):
                            mm(V(pz.t[:, 0:n], [pz.b()]), V(winb.t[:, k, C_POOL + g * 128:C_POOL + (g + 1) * 128], [winb.b(k)]),
                               hT_rhs(k, n), start=(k == 0), stop=(k == 15), signal=(k == 15))
                        yield
                        pbuf = pbufs[g]
                        L = 16 + n
                        cp("pool", pbuf.v((slice(None), slice(0, 16))), phist[g].v())
                        cp("act", pbuf.v((slice(None), slice(16, L))), V(pz.t[:, 0:n], [pz.b()]))
                        cp("pool", phist[g].v(), pbuf.v((slice(None), slice(n, L))))
                        yield
                        cur = pbuf
                        sh = 1
                        lo = 0
                        for lvl in range(g + 1):
                            lo2 = lo + sh
                            dst = ptmp[lvl % 2]
                            tt("pool", dst.v((slice(None), slice(lo2, L))), cur.v((slice(None), slice(lo2, L))),
                               cur.v((slice(None), slice(lo, L - sh))), ALU.add)
                            cur = dst
                            lo = lo2
                            sh *= 2
                            yield
                        d_ = dbf[g % 2]
                        if first_real:
                            tt("dve", cur.v((slice(None), slice(16, L))), cur.v((slice(None), slice(16, L))),
                               V(pinv.t[:, g, 0:n], [pinv.b()]), ALU.mult)
                            tt("dve", d_.v(sl), cur.v((slice(None), slice(16, L))), pbuf.v((slice(None), slice(16, L))),
                               ALU.subtract)
                        else:
                            stt("dve", d_.v(sl), cur.v((slice(None), slice(16, L))), 1.0 / w,
                                pbuf.v((slice(None), slice(16, L))), ALU.mult, ALU.subtract)
                        yield
                        pz2 = next_pz()
                        mm(V(pz2.t[:, 0:n], [pz2.b()]), V(wpool.t[:, g, :], [wpool.b()]), d_.v(sl))
                        yield
                        yc = ycat[g % 2]
                        act(yc.v(sl), V(pz2.t[:, 0:n], [pz2.b()]), AF.Copy, scale=col(O_PS + g))
                        S.dma("sp", V(cat_d.t[:, g, os_:os_ + n], [cat_d.b()]), yc.v(sl))

                    ck(f"p1_c{ci}_pool")
                    zq = []
                    for t in range(4):
                        z = proj(C_Q + t * 128, 128, n, "act" if t % 2 == 0 else "dve")
                        zq.append(z)
                        act(sqs[t % 2].v(sl), z.v(sl), AF.Square)
                        mm(V(pb[6].t[:, 0:n], [pb[6].b()]), ones.v(), sqs[t % 2].v(sl), start=(t == 0), stop=(t == 3),
                           signal=(t == 3))
                    rsqrt_act(rss[1].v(sl), V(pb[6].t[:, 0:n], [pb[6].b()]), 1.0 / 512, epsT.v())
                    for t in range(4):
                        stt("dve", V(qn.t[:, t, 0:n], [qn.b()]), zq[t].v(sl), col(O_GQL + t), rss[1].v(sl),
                            ALU.mult, ALU.mult)
                    S.dma("sp", V(qn_d.t[:, :, os_:os_ + n], [qn_d.b()]), V(qn.t[:, :, 0:n], [qn.b()]))
                    ck(f"p1_c{ci}_q")
                    def xattn_gen(h):
                        z = proj(C_MQ + h * 128, 128, n, "act")
                        yield
                        act(sqs[h % 2].v(sl), z.v(sl), AF.Square)
                        mm(V(pb[6].t[:, 0:n], [pb[6].b()]), ones.v(), sqs[h % 2].v(sl))
                        yield
                        rs = rss[h % 2]
                        rsqrt_act(rs.v(sl), V(pb[6].t[:, 0:n], [pb[6].b()]), 1.0 / 128, epsT.v())
                        yield
                        stt("dve", qx.v(sl), z.v(sl), col(O_GQX), rs.v(sl), ALU.mult, ALU.mult)
                        yield
                        for mb in range(2):
                            pz = next_pz()
                            mm(V(pz.t[:, 0:n], [pz.b()]), V(KxT.t[:, h, mb * 128:(mb + 1) * 128], [KxT.b(h)]), qx.v(sl))
                            act(pexp[mb].v(sl), V(pz.t[:, 0:n], [pz.b()]), AF.Exp, scale=1.0 / math.sqrt(128.0))
                        yield
                        po_ = next_pz()
                        for mb in range(2):
                            mm(V(po_.t[:, 0:n], [po_.b()]), V(Vx.t[:, mb, h * 128:(h + 1) * 128], [Vx.b(mb)]), pexp[mb].v(sl),
                               start=(mb == 0), stop=(mb == 1), signal=(mb == 1))
                        for mb in range(2):
                            mm(V(pb[6].t[:, 0:n], [pb[6].b()]), ones.v(), pexp[mb].v(sl),
                               start=(mb == 0), stop=(mb == 1), signal=(mb == 1))
                        yield
                        recip(rl1.v(sl), V(pb[6].t[:, 0:n], [pb[6].b()]))
                        yield
                        yc = ycx[h % 2]
                        tt("dve", yc.v(sl), V(po_.t[:, 0:n], [po_.b()]), rl1.v(sl), ALU.mult)
                        S.dma("sp", V(cat_d.t[:, 12 + h, os_:os_ + n], [cat_d.b()]), yc.v(sl))

                    for g in range(4):
                        gens = [pool_gen(g), xattn_gen(g)]
                        while gens:
                            for gg in list(gens):
                                try:
                                    next(gg)
                                except StopIteration:
                                    gens.remove(gg)
                if ci + 1 < len(chunks):
                    s_n, n_n = chunks[ci + 1]
                    for b_ in range(n_n // 128):
                        norm_post(xbs[b_])
                        evac_hT(hTs[(ci + 1) % 2], b_ * 128)
                ck(f"p1_c{ci}")
            ck("p1")
            if debug == "p1":
                pass
            S.barrier()

        SC = 1.0 / math.sqrt(192.0)
        with ExitStack() as s2:
            wq_b = sb(s2, [128, 4, 1536], BF16, "wq_b")
            wkv_b = sb(s2, [128, 2, 2048], BF16, "wkv_b")
            qnT = sb(s2, [128, 4, NOWN], BF16, "qnT")
            KT = sb(s2, [128, TK], BF16, "KT")
            Vh = sb(s2, [128, 64, 128], BF16, "Vh")
            QN = sb(s2, [128, NOWN], BF16, "QN")
            QR = sb(s2, [128, NOWN], BF16, "QR")
            S.op("pool", lambda e: e.memset(QR.t[64:128, :], 0.0), writes=[V(None, [QR.b("z")])])
            sck = sb(s2, [128, 64], F32, "sck")
            sstmp = sb(s2, [128, 64], F32, "sstmp")
            sq2 = [sb(s2, [128, 512], BF16, f"sq2_{i}") for i in range(2)]
            sqr2 = sb(s2, [64, 512], BF16, "sqr2")
            sqq = sb(s2, [128, 512], BF16, "sqq")
            cq = sb(s2, [128, 512], F32, "cq")
            qg = sb(s2, [64, 512], F32, "qg")
            cs2 = sb(s2, [64, 512], F32, "cs2")
            sn2 = sb(s2, [64, 512], F32, "sn2")
            u1 = sb(s2, [64, 512], F32, "u1")
            u2 = sb(s2, [64, 512], F32, "u2")
            PT = [sb(s2, [128, 512], BF16, f"PT{i}") for i in range(4)]
            rl2 = sb(s2, [128, 512], F32, "rl2")
            tinyT = sb(s2, [128, 1], F32, "tinyT")
            memset("dve", tinyT.v(), 1e-30)
            oh = [sb(s2, [128, 512], BF16, f"oh{i}") for i in range(2)]
            with ExitStack() as s0:
                stage_w2 = [sb(s0, [128, 2048], F32, f"wst2_{i}") for i in range(2)]
                src_q = w_q_up.t.rearrange("(k p) c -> k p c", p=128)
                src_kv = w_kv_up.t.rearrange("(k p) c -> k p c", p=128)
                for k in range(4):
                    S.dma("sp", stage_w2[k % 2].v((slice(None), slice(0, 1536))), V(src_q[k], []))
                    cp("pool" if k % 2 else "dve", V(wq_b.t[:, k, :], [wq_b.b()]), stage_w2[k % 2].v((slice(None), slice(0, 1536))))
                for k in range(2):
                    S.dma("sp", stage_w2[k % 2].v(), V(src_kv[k], []))
                    cp("pool" if k % 2 else "dve", V(wkv_b.t[:, k, :], [wkv_b.b()]), stage_w2[k % 2].v())
                S.dma("sp", qnT.v(), qn_d.v())
                S.barrier()

            cst_f = [sb(s2, [128, 1408], F32, f"cst_f{i}") for i in range(2)]
            cst_b = [sb(s2, [128, 1408], BF16, f"cst_b{i}") for i in range(2)]

            loaded = [-1]

            def cast_load(i):
                kind, wsrc, wdst, a_, b_ = cast_jobs[i]
                cf = cst_f[i % 2]
                if kind == "gu":
                    S.dma("sp", cf.v(), V(wsrc.t[a_ * 128:(a_ + 1) * 128, b_ * 1408:(b_ + 1) * 1408], []))
                else:
                    S.dma("sp", cf.v((slice(None), slice(0, 1024))),
                          V(wsrc.t[a_ * 128:(a_ + 1) * 128, b_ * 1024:(b_ + 1) * 1024], []))
                loaded[0] = i

            def issue_casts(n):
                for _ in range(n):
                    if cast_pos[0] >= len(cast_jobs):
                        return
                    i = cast_pos[0]
                    kind, wsrc, wdst, a_, b_ = cast_jobs[i]
                    cast_pos[0] += 1
                    cf, cb = cst_f[i % 2], cst_b[i % 2]
                    if loaded[0] < i:
                        cast_load(i)
                    if i + 1 < len(cast_jobs):
                        cast_load(i + 1)
                    if kind == "gu":
                        k, q = a_, b_
                        cp("dve", cb.v(), cf.v())
                        S.dma("sp", V(wdst.t[q * 11:(q + 1) * 11, :, k, :].rearrange("j p c -> p j c"),
                                      [wdst.b(q * 11 + jj) for jj in range(11)]),
                              V(cb.t[:, :].rearrange("p (j c) -> p j c", c=128), [cb.b()]))
                    else:
                        j, hh = a_, b_
                        cp("dve", cb.v((slice(None), slice(0, 1024))), cf.v((slice(None), slice(0, 1024))))
                        S.dma("sp", V(wdst.t[j][:, hh * 1024:(hh + 1) * 1024], [wdst.b(j)]),
                              cb.v((slice(None), slice(0, 1024))))

            qchunks = [(0, 128)] + [(128 + 512 * i, 512) for i in range(8)]
            unit = [0]
            gunit = [0]
            for h in range(8):
                def k_gen():
                    for kc in range(16):
                        pz = pb[kc % 2]
                        for t in range(2):
                            mm(pz.v(), V(wkv_b.t[:, t, h * 256:h * 256 + 128], [wkv_b.b()]),
                               V(nT.t[:, t, kc * 512:(kc + 1) * 512], [nT.b()]), start=(t == 0), stop=(t == 1), signal=(t == 1))
                        act(V(KT.t[:, kc * 512:(kc + 1) * 512], [KT.b(kc)]), pz.v(), AF.Copy, scale=col(O_GKN))
                        sq = sq2[kc % 2]
                        act(sq.v(), pz.v(), AF.Square)
                        for b_ in range(4):
                            kb = kc * 4 + b_
                            mm(V(pb[2].t[:, kb:kb + 1], [pb[2].b()]), sq.v((slice(None), slice(b_ * 128, (b_ + 1) * 128))),
                               ones.v((slice(None), slice(0, 1))), signal=(b_ == 3))
                        yield
                    tt("dve", sstmp.v(), V(pb[2].t[:, 0:64], [pb[2].b()]), sskr.v(), ALU.add)
                    rsqrt_act(sstmp.v(), sstmp.v(), 1.0 / 192, epsT.v())
                    ts("dve", sck.v(), sstmp.v(), SC, ALU.mult)

                def v_gen():
                    for kg4 in range(16):
                        pz = pb[3 + kg4 % 2]
                        for b_ in range(4):
                            kb = kg4 * 4 + b_
                            for t in range(2):
                                mm(V(pz.t[:, b_ * 128:(b_ + 1) * 128], [pz.b()]), V(nT.t[:, t, kb * 128:(kb + 1) * 128], [nT.b()]),
                                   V(wkv_b.t[:, t, h * 256 + 128:h * 256 + 256], [wkv_b.b()]), start=(t == 0), stop=(t == 1),
                                   signal=(b_ == 3 and t == 1))
                        cp("dve" if kg4 % 2 else "act", V(Vh.t[:, kg4 * 4:(kg4 + 1) * 4, :], [Vh.b(kg4)]),
                           V(pz.t[:].rearrange("p (k c) -> p k c", c=128), [pz.b()]))
                        yield

                def q_gen():
                    for (qs, nq) in qchunks:
                        sl = (slice(None), slice(0, nq))
                        sl64 = (slice(0, 64), slice(0, nq))
                        S.dma("sp", cs2.v(sl64), V(cosT.t[:, OWN0 + qs:OWN0 + qs + nq], []))
                        S.dma("sp", sn2.v(sl64), V(sinT.t[:, OWN0 + qs:OWN0 + qs + nq], []))
                        pqn, pqr, pss = pb[5], pb[6], pb[7]
                        prot = pb[6]
                        for t in range(4):
                            mm(V(pqn.t[:, 0:nq], [pqn.b()]), V(wq_b.t[:, t, h * 192:h * 192 + 128], [wq_b.b()]),
                               V(qnT.t[:, t, qs:qs + nq], [qnT.b()]), start=(t == 0), stop=(t == 3), signal=(t == 3))
                        for t in range(4):
                            mm(V(pqr.t[0:64, 0:nq], [pqr.b()]), V(wq_b.t[:, t, h * 192 + 128:h * 192 + 192], [wq_b.b()]),
                               V(qnT.t[:, t, qs:qs + nq], [qnT.b()]), start=(t == 0), stop=(t == 3), signal=(t == 3))
                        yield
                        act(sqq.v(sl), V(pqn.t[:, 0:nq], [pqn.b()]), AF.Square)
                        act(sqr2.v(sl64), V(pqr.t[0:64, 0:nq], [pqr.b()]), AF.Square)
                        act(qg.v(sl64), V(pqr.t[0:64, 0:nq], [pqr.b()]), AF.Copy, scale=col(O_GQR, 1, 64))
                        yield
                        mm(V(pss.t[:, 0:nq], [pss.b()]), ones.v(), sqq.v(sl), start=True, stop=False, signal=False)
                        mm(V(pss.t[:, 0:nq], [pss.b()]), ones.v((slice(0, 64), slice(None))), sqr2.v(sl64), start=False, stop=True)
                        mm(V(prot.t[0:64, 0:nq], [prot.b()]), rotT.v(), qg.v(sl64))
                        yield
                        rsqrt_act(cq.v(sl), V(pss.t[:, 0:nq], [pss.b()]), 1.0 / 192, epsT.v())
                        tt("dve", u1.v(sl64), qg.v(sl64), cs2.v(sl64), ALU.mult)
                        tt("dve", u2.v(sl64), V(prot.t[0:64, 0:nq], [prot.b()]), sn2.v(sl64), ALU.mult)
                        yield
                        stt("dve", V(QN.t[:, qs:qs + nq], [QN.b(qs)]), V(pqn.t[:, 0:nq], [pqn.b()]), col(O_GQN), cq.v(sl),
                            ALU.mult, ALU.mult)
                        tt("dve", u1.v(sl64), u1.v(sl64), u2.v(sl64), ALU.add)
                        tt("dve", V(QR.t[0:64, qs:qs + nq], [QR.b(qs)]), u1.v(sl64), cq.v(sl64), ALU.mult)
                        yield

                gens = [k_gen(), v_gen(), q_gen()]
                while gens:
                    for gg in list(gens):
                        try:
                            next(gg)
                        except StopIteration:
                            gens.remove(gg)
                units = []
                for ci, (qs, nq) in enumerate(qchunks):
                    j0 = qs // 128
                    nqb = nq // 128
                    nfull = 31 + j0
                    order = [(0, None)] + [(31 + j0 + i, i) for i in range(nqb)] + [(kb, None) for kb in range(1, nfull)]
                    for oi, (kb, di) in enumerate(order):
                        units.append((ci, qs, nq, kb, di, oi == 0, oi == len(order) - 1))
                LA = 3
                for ui in range(len(units) + LA):
                    back = None
                    if ui >= LA:
                        uj = ui - LA
                        ci_b, qs_b, nq_b, kb_b, di_b, first_b, last_b = units[uj]
                        c0_b = 0 if di_b is None else di_b * 128
                        ptb = PT[uj % 4]
                        po_ = pb[4 + ci_b % 2]
                        pl_ = pb[6 + ci_b % 2]
                        cslb = (slice(None), slice(c0_b, nq_b))
                        S.prewait("pe", reads=[ptb.v(cslb), V(None, [Vh.b(kb_b // 4)])],
                                  writes=[V(None, [po_.b()]), V(None, [pl_.b()])])
                        back = True
                    if ui < len(units):
                        ci, qs, nq, kb, di, first, last = units[ui]
                        c0 = 0 if di is None else di * 128
                        ps_ = pb[ui % 4]
                        pt_ = PT[ui % 4]
                        csl = (slice(None), slice(c0, nq))
                        mm(V(ps_.t[:, c0:nq], [ps_.b()]), V(KT.t[:, kb * 128:(kb + 1) * 128], [KT.b(kb // 4)]),
                           V(QN.t[:, qs + c0:qs + nq], [QN.b(qs)]), start=True, stop=False, signal=False)
                        mm(V(ps_.t[:, c0:nq], [ps_.b()]), V(krT.t[:, kb * 128:(kb + 1) * 128], [krT.b()]),
                           V(QR.t[:, qs + c0:qs + nq], [QR.b(qs)]), start=False, stop=True)
                        act(pt_.v(csl), V(ps_.t[:, c0:nq], [ps_.b()]), AF.Exp,
                            scale=sck.v((slice(None), slice(kb, kb + 1))), bias=kbias.v((slice(None), slice(kb, kb + 1))))
                        if di is not None:
                            tt("dve", pt_.v((slice(None), slice(c0, c0 + 128))), pt_.v((slice(None), slice(c0, c0 + 128))),
                               tri.v(), ALU.mult)
                    gunit[0] += 1
                    if gunit[0] % 15 == 0:
                        issue_casts(1)
                    if back:
                        mm(V(po_.t[:, c0_b:nq_b], [po_.b()]), V(Vh.t[:, kb_b, :], [Vh.b(kb_b // 4)]), ptb.v(cslb),
                           start=first_b, stop=last_b, signal=False)
                        mm(V(pl_.t[:, c0_b:nq_b], [pl_.b()]), ones.v(), ptb.v(cslb), start=first_b, stop=last_b)
                        if last_b:
                            sl = (slice(None), slice(0, nq_b))
                            act(rl2.v(sl), V(pl_.t[:, 0:nq_b], [pl_.b()]), AF.Ln, bias=tinyT.v())
                            act(rl2.v(sl), rl2.v(sl), AF.Exp, scale=-1.0)
                            o_ = oh[ci_b % 2]
                            tt("dve", o_.v(sl), V(po_.t[:, 0:nq_b], [po_.b()]), rl2.v(sl), ALU.mult)
                            S.dma("sp", V(cat_d.t[:, 4 + h, qs_b:qs_b + nq_b], [cat_d.b()]), o_.v(sl))
                            ck(f"p2_h{h}_c{ci_b}")
                ck(f"p2_h{h}")
            issue_casts(1000)
            S.barrier()
        p12.close()

        with ExitStack() as s3:
            wo_sb = sb(s3, [128, 16, D], BF16, "wo_sb")
            uT = sb(s3, [128, NFT, 512], BF16, "uT")
            h2T = sb(s3, [128, 16, 512], BF16, "h2T")
            h2halo = sb(s3, [128, 16, 2], BF16, "h2halo")
            catc = [sb(s3, [128, 16, 128], BF16, f"catc{i}") for i in range(2)]
            wgs = [sb(s3, [128, 16, 128], BF16, f"wgs{i}") for i in range(2)]
            wus = [sb(s3, [128, 16, 128], BF16, f"wus{i}") for i in range(2)]
            wds = [sb(s3, [128, 4, 512], BF16, f"wds{i}") for i in range(3)]
            xt3 = sb(s3, [128, D], F32, "xt3")
            xb3 = sb(s3, [128, D], BF16, "xb3")
            ss3 = sb(s3, [128, 1], F32, "ss3")
            rs3 = sb(s3, [128, 1], F32, "rs3")
            gbuf = sb(s3, [128, 514], F32, "gbuf")
            acc = sb(s3, [128, 512], F32, "acc")
            sg = sb(s3, [128, 512], F32, "sg")
            ghist = sb(s3, [128, NFT, 2], F32, "ghist")
            outs = [sb(s3, [128, 512], F32, f"outs{i}") for i in range(2)]
            x1r = [sb(s3, [128, 512], F32, f"x1r{i}") for i in range(2)]
            with ExitStack() as s0:
                stage_w3 = [sb(s0, [128, 1024], F32, f"wst3_{i}") for i in range(2)]
                src_o = w_o.t.rearrange("(k p) c -> k p c", p=128)
                for k2 in range(32):
                    k, hh = k2 // 2, k2 % 2
                    S.dma("sp", stage_w3[k2 % 2].v(), V(src_o[k][:, hh * 1024:(hh + 1) * 1024], []))
                    cp("pool" if k2 % 2 else "dve", V(wo_sb.t[:, k, hh * 1024:(hh + 1) * 1024], [wo_sb.b((k, hh))]),
                       stage_w3[k2 % 2].v())
                S.barrier()

            x1r += [sb(s3, [128, 512], F32, f"x1r{i}") for i in (2, 3)]
            wds.append(sb(s3, [128, 4, 512], BF16, "wds3"))
            gcolb = V(cols.t[:, O_GFFN:O_GFFN + 16].unsqueeze(2).to_broadcast([128, 16, 128]), [cols.b()])
            gcolb8a = V(cols.t[:, O_GFFN:O_GFFN + 8].unsqueeze(2).to_broadcast([128, 8, 128]), [cols.b()])
            gcolb8b = V(cols.t[:, O_GFFN + 8:O_GFFN + 16].unsqueeze(2).to_broadcast([128, 8, 128]), [cols.b()])
            p3chunks = [(0, 128)] + [(128 + 512 * i, 512) for i in range(8)]
            wcnt = [0]
            blk3 = [0]
            for ci, (cs, n) in enumerate(p3chunks):
                nb_ = n // 128
                for b_ in range(nb_):
                    r0 = cs + b_ * 128
                    cc = catc[blk3[0] % 2]
                    blk3[0] += 1
                    S.dma("sp", cc.v(), V(cat_d.t[:, :, r0:r0 + 128], [cat_d.b()]))
                    S.dma("sp", xt3.v(), V(xk.t[OWN0 + r0:OWN0 + r0 + 128, :], []))
                    for dc in range(4):
                        pw = pb[2 + dc]
                        for f in range(16):
                            mm(pw.v(), V(cc.t[:, f, :], [cc.b()]), V(wo_sb.t[:, f, dc * 512:(dc + 1) * 512], [wo_sb.b((f, dc // 2))]),
                               start=(f == 0), stop=(f == 15), signal=(f == 15))
                        tt("dve", xt3.v((slice(None), slice(dc * 512, (dc + 1) * 512))), pw.v(),
                           xt3.v((slice(None), slice(dc * 512, (dc + 1) * 512))), ALU.add)
                    S.dma("sp", V(x1_d.t[r0:r0 + 128, :], [x1_d.b()]), xt3.v())
                    act(xb3.v(), xt3.v(), AF.Square, accum=ss3.v())
                    rsqrt_act(rs3.v(), ss3.v(), 1.0 / D, epsT.v())
                    ts("dve", xb3.v(), xt3.v(), rs3.v(), ALU.mult)
                    for k in range(16):
                        bank = pb[k // 8]
                        tr(V(pbf(k // 8)[:, k % 8, :], [bank.b()]), xb3.v((slice(None), slice(k * 128, (k + 1) * 128))),
                           idb.v(), signal=(k % 8 == 7))
                    c0 = b_ * 128
                    tt("dve", V(h2T.t[:, 0:8, c0:c0 + 128], [h2T.b(("a", c0))]), V(pbf(0), [pb[0].b()]), gcolb8a, ALU.mult)
                    tt("dve", V(h2T.t[:, 8:16, c0:c0 + 128], [h2T.b(("b", c0))]), V(pbf(1), [pb[1].b()]), gcolb8b, ALU.mult)
                if ci == 0:
                    cp("dve", h2halo.v(), V(h2T.t[:, :, 126:128], [h2T.b(("a", 0)), h2T.b(("b", 0))]))
                    ck("p3_c0")
                    continue

                def h2_rhs(k):
                    return V(h2T.t[:, k, :], [h2T.b((("a" if k < 8 else "b"), c0)) for c0 in range(0, 512, 128)])

                for j in range(NFT):
                    wg_ = wgs[j % 2]
                    wu_ = wus[j % 2]
                    S.dma("sp", wg_.v(), V(wg_b.t[j], [wg_b.b(j)]))
                    S.dma("sp", wu_.v(), V(wu_b.t[j], [wu_b.b(j)]))
                    if ci == 1:
                        ph = pb[6]
                        for k in range(16):
                            mm(V(ph.t[:, 0:2], [ph.b()]), V(wg_.t[:, k, :], [wg_.b()]), V(h2halo.t[:, k, :], [h2halo.b()]),
                               start=(k == 0), stop=(k == 15), signal=(k == 15))
                        act(V(ghist.t[:, j, :], [ghist.b(j)]), V(ph.t[:, 0:2], [ph.b()]), AF.Copy, scale=halo.v())
                    pg = pb[2 + j % 2]
                    pu = pb[4 + j % 2]
                    for k in range(16):
                        mm(pg.v(), V(wg_.t[:, k, :], [wg_.b()]), h2_rhs(k), start=(k == 0), stop=(k == 15), signal=(k == 15))
                    for k in range(16):
                        mm(pu.v(), V(wu_.t[:, k, :], [wu_.b()]), h2_rhs(k), start=(k == 0), stop=(k == 15), signal=(k == 15))
                    cp("pool", gbuf.v((slice(None), slice(0, 2))), V(ghist.t[:, j, :], [ghist.b(j)]))
                    cp("act", gbuf.v((slice(None), slice(2, 514))), pg.v())
                    act(acc.v(), pg.v(), AF.Identity, scale=col(O_CW + 2 * NFT + j), bias=col(O_CB + j))
                    cp("pool", V(ghist.t[:, j, :], [ghist.b(j)]), gbuf.v((slice(None), slice(512, 514))))
                    stt("dve", acc.v(), gbuf.v((slice(None), slice(1, 513))), col(O_CW + NFT + j), acc.v(), ALU.mult, ALU.add)
                    stt("dve", acc.v(), gbuf.v((slice(None), slice(0, 512))), col(O_CW + j), acc.v(), ALU.mult, ALU.add)
                    act(sg.v(), acc.v(), AF.Silu)
                    tt("dve", V(uT.t[:, j, :], [uT.b(j)]), sg.v(), pu.v(), ALU.mult)
                for dc in range(4):
                    for b_ in range(4):
                        r0 = cs + b_ * 128
                        S.dma("sp", x1r[b_].v(), V(x1_d.t[r0:r0 + 128, dc * 512:(dc + 1) * 512], [x1_d.b()]))
                    for jg in range(NFT // 4):
                        wd_ = wds[wcnt[0] % 4]
                        wcnt[0] += 1
                        S.dma("sp", wd_.v(), V(wd_b.t[jg * 4:(jg + 1) * 4, :, dc * 512:(dc + 1) * 512].rearrange("j p c -> p j c"),
                                               [wd_b.b(jg * 4 + i) for i in range(4)]))
                        for jj in range(4):
                            j = jg * 4 + jj
                            for b_ in range(4):
                                pd = pb[(0, 1, 6, 7)[b_]]
                                mm(pd.v(), V(uT.t[:, j, b_ * 128:(b_ + 1) * 128], [uT.b(j)]), V(wd_.t[:, jj, :], [wd_.b()]),
                                   start=(j == 0), stop=(j == NFT - 1), signal=(j == NFT - 1 or (jj == 3 and b_ == 3)))
                    for b_ in range(4):
                        pd = pb[(0, 1, 6, 7)[b_]]
                        r0 = cs + b_ * 128
                        xr = x1r[b_]
                        o_ = outs[b_ % 2]
                        tt("dve", o_.v(), pd.v(), xr.v(), ALU.add)
                        S.dma("act", V(out_t.t[r0 - 128:r0, dc * 512:(dc + 1) * 512], [out_t.b()]), o_.v())
                ck(f"p3_c{ci}")
            S.finish()


_NC_CACHE = {}


def _host_tables(core):
    half = 32
    inv_freq = (1.0 / (10000.0 ** (np.arange(half, dtype=np.float32) / np.float32(half)))).astype(np.float32)
    base = -4096 if core == 0 else 0
    pos = (np.arange(TK) + base).astype(np.float32)
    ang = pos[None, :] * inv_freq[:, None]
    cos = np.cos(ang).astype(np.float32)
    sin = np.sin(ang).astype(np.float32)
    cosT = np.concatenate([cos, cos], 0)
    sinT = np.concatenate([sin, sin], 0)
    kbias = np.zeros((128, 64), np.float32)
    if core == 0:
        kbias[:, :32] = -30000.0
    halo = np.full((128, 1), 0.0 if core == 0 else 1.0, np.float32)
    pinv = np.zeros((128, 4, P1N), np.float32)
    t = np.arange(P1N)
    for g, w in enumerate((2, 4, 8, 16)):
        cnt = np.minimum(t + 1, w) if core == 0 else np.full(P1N, w)
        pinv[:, g, :] = (1.0 / cnt.astype(np.float32))[None, :]
    return cosT, sinT, kbias, halo, pinv


def kernel(**inputs):
    x = np.asarray(inputs["x"], np.float32)
    mem = np.asarray(inputs["mem"], np.float32)
    if "nc" not in _NC_CACHE:
        _NC_CACHE["nc"] = build_nc()
    nc = _NC_CACHE["nc"]
    rot = np.zeros((64, 64), np.float32)
    for m in range(32):
        rot[m + 32, m] = -1.0
        rot[m, m + 32] = 1.0
    tri = np.triu(np.ones((128, 128), np.float32))
    shared = {
        "ident_f": np.eye(128, dtype=np.float32), "rotT": rot, "tri": tri,
    }
    for name in ("g_mix", "g_ffn", "g_mem", "g_q_lat", "g_kv_lat", "g_q_mla", "g_k_mla", "pool_scale", "g_q_x", "g_k_x",
                 "conv_w", "conv_b", "w_in", "w_q_up", "w_kv_up", "w_pool", "w_mem_kv", "w_o", "w_gate", "w_up", "w_down"):
        shared[name] = np.ascontiguousarray(np.asarray(inputs[name], np.float32)[0])
    tabs = [_host_tables(0), _host_tables(1)]
    in_maps = []
    for b in range(NB):
        for c in range(2):
            if c == 0:
                xk = np.concatenate([np.zeros((4096, D), np.float32), x[b, :4096]], 0)
            else:
                xk = x[b]
            cosT, sinT, kbias, halo, pinv = tabs[c]
            m = dict(shared)
            m.update({"xk": np.ascontiguousarray(xk), "mem": np.ascontiguousarray(mem[b]), "cosT": cosT, "sinT": sinT,
                      "kbias": kbias, "halo_flag": halo, "pool_inv": pinv})
            in_maps.append(m)
    res = run_bass_kernel_spmd(nc, in_maps, core_ids=list(range(8)))
    out = np.empty((NB, SEQ, D), np.float32)
    for b in range(NB):
        for c in range(2):
            out[b, c * 4096:(c + 1) * 4096] = np.asarray(res.results[b * 2 + c]["out"], np.float32)
    return out
```
